# Optimizing a Trainium2 kernel written in Bass

```python
import jax, jax.numpy as jnp
from jax import lax
import numpy as np

D_MODEL = 1024
BATCH = 8
SEQ = 2048
DEPTH = 4

GRID_W = 64
CTX_LEN = 256
N_EVEN = (DEPTH + 1) // 2
N_ODD = DEPTH // 2

MLA_HEADS = 8
MLA_Q_RANK = 256
MLA_KV_RANK = 128
MLA_NOPE = 64
MLA_ROPE = 32
MLA_V = 64
MLA_SCALE = (MLA_NOPE + MLA_ROPE) ** -0.5
Q_BLOCK = 128
ROPE_BASE = 10000.0
GLA_HEADS = 4
GLA_DK = 64
GLA_DV = 128
GLA_GATE_RANK = 16
GLA_GATE_NORM = 16.0
GLA_CHUNK = 64
CONV_CH = 768
CONV_W = 31
FNET_GROUPS = 4
FNET_GROUP_CH = 64
FNET_CH = FNET_GROUPS * FNET_GROUP_CH
D_FF = 2816
FFN_CONV_W = 3
EPS = 1e-6

EVEN_SIZES = [MLA_Q_RANK, MLA_KV_RANK, MLA_ROPE,
              GLA_HEADS * GLA_DK, GLA_HEADS * GLA_DK, GLA_HEADS * GLA_DV,
              GLA_GATE_RANK, GLA_GATE_RANK, GLA_HEADS * GLA_DV]
EVEN_IN = int(sum(EVEN_SIZES))
EVEN_OFFSETS = [int(v) for v in np.cumsum(EVEN_SIZES)[:-1]]
ODD_IN = 2 * CONV_CH + FNET_CH

kernel_name = "hybrid_mla_gla_conformer_fnet_dit"


def rms_norm(x, g):
    xf = x.astype(jnp.float32)
    y = xf * lax.rsqrt(jnp.mean(xf * xf, axis=-1, keepdims=True) + EPS)
    return (y * g.astype(jnp.float32)).astype(x.dtype)


def layer_norm(x, g, b):
    xf = x.astype(jnp.float32)
    mu = jnp.mean(xf, axis=-1, keepdims=True)
    xc = xf - mu
    y = xc * lax.rsqrt(jnp.mean(xc * xc, axis=-1, keepdims=True) + EPS)
    return (y * g.astype(jnp.float32) + b.astype(jnp.float32)).astype(x.dtype)


def dwconv(x, w, b):
    k, ch = w.shape
    pad = (k - 1) // 2
    y = lax.conv_general_dilated(x, w[:, None, :].astype(x.dtype), window_strides=(1,),
                                 padding=[(pad, pad)], dimension_numbers=("NWC", "WIO", "NWC"),
                                 feature_group_count=ch)
    return y + b.astype(x.dtype)


def to_heads(t, h):
    bsz, n, _ = t.shape
    return t.reshape(bsz, n, h, -1).transpose(0, 2, 1, 3)


def merge_heads(t):
    bsz, h, n, d = t.shape
    return t.transpose(0, 2, 1, 3).reshape(bsz, n, h * d)


def axial_rope_tables(n, dtype):
    rows = n // GRID_W
    row = jnp.repeat(jnp.arange(rows), GRID_W).astype(jnp.float32)
    col = jnp.tile(jnp.arange(GRID_W), rows).astype(jnp.float32)
    half = MLA_ROPE // 2
    inv = ROPE_BASE ** (-jnp.arange(0, half, 2, dtype=jnp.float32) / half)
    ar = row[:, None] * inv
    ac = col[:, None] * inv
    ang = jnp.concatenate([ar, ar, ac, ac], axis=-1)
    return jnp.cos(ang).astype(dtype), jnp.sin(ang).astype(dtype)


def rope_2d(x, cos, sin):
    x0, x1, x2, x3 = jnp.split(x, 4, axis=-1)
    rot = jnp.concatenate([-x1, x0, -x3, x2], axis=-1)
    return x * cos + rot * sin


def mla_queries(qc, q_norm, w_uq):
    bsz, n, _ = qc.shape
    q = (rms_norm(qc, q_norm) @ w_uq).reshape(bsz, n, MLA_HEADS, MLA_NOPE + MLA_ROPE).transpose(0, 2, 1, 3)
    return q[..., :MLA_NOPE], q[..., MLA_NOPE:]


def mla_keys(kvc, kv_norm, w_ukv):
    bsz, n, _ = kvc.shape
    kv = (rms_norm(kvc, kv_norm) @ w_ukv).reshape(bsz, n, MLA_HEADS, MLA_NOPE + MLA_V).transpose(0, 2, 1, 3)
    return kv[..., :MLA_NOPE], kv[..., MLA_NOPE:]


def mla_attend(qn, qr, kn, kr, v):
    s = jnp.einsum("bhqd,bhkd->bhqk", qn, kn) + jnp.einsum("bhqr,bkr->bhqk", qr, kr)
    p = jax.nn.softmax(s.astype(jnp.float32) * MLA_SCALE, axis=-1).astype(v.dtype)
    return jnp.einsum("bhqk,bhkd->bhqd", p, v)


def mla_attend_blocks(qn, qr, kn, kr, v):
    bsz, h, n, _ = qn.shape
    nb = n // Q_BLOCK
    qn_b = qn.reshape(bsz, h, nb, Q_BLOCK, -1).transpose(2, 0, 1, 3, 4)
    qr_b = qr.reshape(bsz, h, nb, Q_BLOCK, -1).transpose(2, 0, 1, 3, 4)
    o = lax.map(lambda a: mla_attend(a[0], a[1], kn, kr, v), (qn_b, qr_b))
    return o.transpose(1, 2, 0, 3, 4).reshape(bsz, h, n, -1)


def gla_chunk_scan(q, k, v, lg, s0):
    bsz, h, n, dk = q.shape
    dv = v.shape[-1]
    nc = n // GLA_CHUNK
    cs = GLA_CHUNK
    lower = jnp.tril(jnp.ones((cs, cs), dtype=bool))[:, :, None]

    def chunks(t):
        return t.reshape(bsz, h, nc, cs, t.shape[-1]).transpose(2, 0, 1, 3, 4)

    def step(state, inp):
        qc, kc, vc, gc = inp
        b = jnp.cumsum(gc, axis=2)
        o_inter = jnp.einsum("bhid,bhde->bhie", qc * jnp.exp(b), state)
        diff = b[:, :, :, None, :] - b[:, :, None, :, :]
        decay = jnp.where(lower, jnp.exp(jnp.where(lower, diff, 0.0)), 0.0)
        a = jnp.einsum("bhid,bhjd,bhijd->bhij", qc, kc, decay)
        o_intra = jnp.einsum("bhij,bhje->bhie", a, vc)
        b_end = b[:, :, -1:, :]
        state = jnp.exp(b_end[:, :, 0, :])[..., None] * state + jnp.einsum(
            "bhjd,bhje->bhde", kc * jnp.exp(b_end - b), vc)
        return state, o_inter + o_intra

    s_fin, o = lax.scan(step, s0, (chunks(q), chunks(k), chunks(v), chunks(lg)))
    return o.transpose(1, 2, 0, 3, 4).reshape(bsz, h, n, dv), s_fin


def _flip(t):
    return jnp.flip(t, axis=2)


def gla_features(gq, gk, gv, g_lr, w_g, b_g):
    f32 = jnp.float32
    q = to_heads(gq, GLA_HEADS).astype(f32) * (GLA_DK ** -0.5)
    k = to_heads(gk, GLA_HEADS).astype(f32)
    v = to_heads(gv, GLA_HEADS).astype(f32)
    lg = to_heads(jax.nn.log_sigmoid((g_lr @ w_g + b_g).astype(f32)) / GLA_GATE_NORM, GLA_HEADS)
    return q, k, v, lg


def gla_output(o, gr, o_norm):
    bsz, h, n, dv = o.shape
    o = o.transpose(0, 2, 1, 3).astype(gr.dtype)
    y = rms_norm(o, o_norm) * jax.nn.silu(gr.reshape(bsz, n, h, dv))
    return y.reshape(bsz, n, h * dv)


def even_mixer(h_lat, h_ctx, cos, sin, need_ctx, w_in, q_norm, kv_norm, w_uq, w_ukv,
               w_gfw, b_gfw, w_gbw, b_gbw, o_norm, w_out):
    pl = jnp.split(h_lat @ w_in, EVEN_OFFSETS, axis=-1)
    pc = jnp.split(h_ctx @ w_in, EVEN_OFFSETS, axis=-1)
    qn_l, qr_l = mla_queries(pl[0], q_norm, w_uq)
    qr_l = rope_2d(qr_l, cos, sin)
    kn_l, v_l = mla_keys(pl[1], kv_norm, w_ukv)
    kr_l = rope_2d(pl[2], cos, sin)
    kn_c, v_c = mla_keys(pc[1], kv_norm, w_ukv)
    kr_c = pc[2]
    kn_all = jnp.concatenate([kn_c, kn_l], axis=2)
    kr_all = jnp.concatenate([kr_c, kr_l], axis=1)
    v_all = jnp.concatenate([v_c, v_l], axis=2)
    a_lat = merge_heads(mla_attend_blocks(qn_l, qr_l, kn_all, kr_all, v_all))
    q_l, k_l, vv_l, lf_l = gla_features(pl[3], pl[4], pl[5], pl[6], w_gfw, b_gfw)
    lb_l = to_heads(jax.nn.log_sigmoid((pl[7] @ w_gbw + b_gbw).astype(jnp.float32)) / GLA_GATE_NORM, GLA_HEADS)
    q_c, k_c, vv_c, lf_c = gla_features(pc[3], pc[4], pc[5], pc[6], w_gfw, b_gfw)
    lb_c = to_heads(jax.nn.log_sigmoid((pc[7] @ w_gbw + b_gbw).astype(jnp.float32)) / GLA_GATE_NORM, GLA_HEADS)
    zeros = jnp.zeros((h_lat.shape[0], GLA_HEADS, GLA_DK, GLA_DV), jnp.float32)
    o_cf, s_cf = gla_chunk_scan(q_c, k_c, vv_c, lf_c, zeros)
    o_lf, _ = gla_chunk_scan(q_l, k_l, vv_l, lf_l, s_cf)
    o_cb, s_cb = gla_chunk_scan(_flip(q_c), _flip(k_c), _flip(vv_c), _flip(lb_c), zeros)
    o_lb, _ = gla_chunk_scan(_flip(q_l), _flip(k_l), _flip(vv_l), _flip(lb_l), s_cb)
    g_lat = gla_output(o_lf + _flip(o_lb), pl[8], o_norm)
    y_lat = jnp.concatenate([a_lat, g_lat], axis=-1) @ w_out
    y_ctx = None
    if need_ctx:
        qn_c, qr_c = mla_queries(pc[0], q_norm, w_uq)
        a_ctx = merge_heads(mla_attend(qn_c, qr_c, kn_c, kr_c, v_c))
        g_ctx = gla_output(o_cf + _flip(o_cb), pc[8], o_norm)
        y_ctx = jnp.concatenate([a_ctx, g_ctx], axis=-1) @ w_out
    return y_lat, y_ctx


def odd_mixer(h, w_in, dw_w, dw_b, ln_g, ln_b, w_out):
    a, gt, f = jnp.split(h @ w_in, [CONV_CH, 2 * CONV_CH], axis=-1)
    u = a * jax.nn.sigmoid(gt)
    u = jax.nn.silu(layer_norm(dwconv(u, dw_w, dw_b), ln_g, ln_b))
    bsz, n, _ = f.shape
    fg = f.reshape(bsz, n, FNET_GROUPS, FNET_GROUP_CH).astype(jnp.float32)
    fm = jnp.fft.fftn(fg, axes=(1, 3), norm="ortho").real.reshape(bsz, n, FNET_CH).astype(h.dtype)
    return jnp.concatenate([u, fm], axis=-1) @ w_out


def conv_ffn(h, w_in, dw_w, dw_b, w_out):
    g, v = jnp.split(h @ w_in, 2, axis=-1)
    return (jax.nn.gelu(dwconv(g, dw_w, dw_b), approximate=False) * v) @ w_out


def setup_inputs(seed: int = 0) -> dict:
    key = jax.random.key(seed)
    it = iter(list(jax.random.split(key, 40)))
    D = D_MODEL

    def nrm(shape, scale):
        return jax.random.normal(next(it), shape, jnp.float32) * scale

    def gain(shape):
        return 1.0 + nrm(shape, 0.02)

    return {
        "x": nrm((BATCH, SEQ, D), 1.0),
        "c": nrm((BATCH, D), 1.0),
        "ctx": nrm((BATCH, CTX_LEN, D), 1.0),
        "c_ctx": nrm((D,), 1.0),
        "mod_w": nrm((DEPTH, D, 6 * D), D ** -0.5),
        "mod_b": nrm((DEPTH, 6 * D), 0.02),
        "pre_mix_g": gain((DEPTH, D)),
        "post_mix_g": gain((DEPTH, D)),
        "pre_ffn_g": gain((DEPTH, D)),
        "post_ffn_g": gain((DEPTH, D)),
        "ev_in_w": nrm((N_EVEN, D, EVEN_IN), D ** -0.5),
        "mla_q_norm": gain((N_EVEN, MLA_Q_RANK)),
        "mla_kv_norm": gain((N_EVEN, MLA_KV_RANK)),
        "mla_w_uq": nrm((N_EVEN, MLA_Q_RANK, MLA_HEADS * (MLA_NOPE + MLA_ROPE)), MLA_Q_RANK ** -0.5),
        "mla_w_ukv": nrm((N_EVEN, MLA_KV_RANK, MLA_HEADS * (MLA_NOPE + MLA_V)), MLA_KV_RANK ** -0.5),
        "gla_w_gate_fw": nrm((N_EVEN, GLA_GATE_RANK, GLA_HEADS * GLA_DK), GLA_GATE_RANK ** -0.5),
        "gla_b_gate_fw": nrm((N_EVEN, GLA_HEADS * GLA_DK), 0.5),
        "gla_w_gate_bw": nrm((N_EVEN, GLA_GATE_RANK, GLA_HEADS * GLA_DK), GLA_GATE_RANK ** -0.5),
        "gla_b_gate_bw": nrm((N_EVEN, GLA_HEADS * GLA_DK), 0.5),
        "gla_o_norm": gain((N_EVEN, GLA_DV)),
        "ev_out_w": nrm((N_EVEN, D, D), D ** -0.5),
        "od_in_w": nrm((N_ODD, D, ODD_IN), D ** -0.5),
        "conf_dw_w": nrm((N_ODD, CONV_W, CONV_CH), CONV_W ** -0.5),
        "conf_dw_b": nrm((N_ODD, CONV_CH), 0.02),
        "conf_ln_g": gain((N_ODD, CONV_CH)),
        "conf_ln_b": nrm((N_ODD, CONV_CH), 0.02),
        "od_out_w": nrm((N_ODD, D, D), D ** -0.5),
        "ffn_in_w": nrm((DEPTH, D, 2 * D_FF), D ** -0.5),
        "ffn_dw_w": nrm((DEPTH, FFN_CONV_W, D_FF), FFN_CONV_W ** -0.5),
        "ffn_dw_b": nrm((DEPTH, D_FF), 0.02),
        "ffn_out_w": nrm((DEPTH, D_FF, D), D_FF ** -0.5),
    }


def reference(x, c, ctx, c_ctx, mod_w, mod_b, pre_mix_g, post_mix_g, pre_ffn_g, post_ffn_g,
              ev_in_w, mla_q_norm, mla_kv_norm, mla_w_uq, mla_w_ukv,
              gla_w_gate_fw, gla_b_gate_fw, gla_w_gate_bw, gla_b_gate_bw, gla_o_norm, ev_out_w,
              od_in_w, conf_dw_w, conf_dw_b, conf_ln_g, conf_ln_b, od_out_w,
              ffn_in_w, ffn_dw_w, ffn_dw_b, ffn_out_w):
    n = x.shape[1]
    cos, sin = axial_rope_tables(n, x.dtype)
    last_ctx_reader = ((DEPTH - 1) // 2) * 2
    for l in range(DEPTH):
        need_ctx = l < last_ctx_reader
        use_ctx = need_ctx or (l % 2 == 0)
        i = l // 2
        m_lat = (jax.nn.silu(c) @ mod_w[l] + mod_b[l])[:, None, :]
        sh1, sc1, g1, sh2, sc2, g2 = jnp.split(m_lat, 6, axis=-1)
        h_lat = rms_norm(x, pre_mix_g[l]) * (1 + sc1) + sh1
        h_ctx = None
        if use_ctx:
            m_ctx = jax.nn.silu(c_ctx) @ mod_w[l] + mod_b[l]
            csh1, csc1, cg1, csh2, csc2, cg2 = jnp.split(m_ctx, 6, axis=-1)
            h_ctx = rms_norm(ctx, pre_mix_g[l]) * (1 + csc1) + csh1
        if l % 2 == 0:
            y_lat, y_ctx = even_mixer(h_lat, h_ctx, cos, sin, need_ctx, ev_in_w[i], mla_q_norm[i],
                                      mla_kv_norm[i], mla_w_uq[i], mla_w_ukv[i],
                                      gla_w_gate_fw[i], gla_b_gate_fw[i], gla_w_gate_bw[i],
                                      gla_b_gate_bw[i], gla_o_norm[i], ev_out_w[i])
        else:
            y_lat = odd_mixer(h_lat, od_in_w[i], conf_dw_w[i], conf_dw_b[i], conf_ln_g[i],
                              conf_ln_b[i], od_out_w[i])
            y_ctx = None
            if need_ctx:
                y_ctx = odd_mixer(h_ctx, od_in_w[i], conf_dw_w[i], conf_dw_b[i], conf_ln_g[i],
                                  conf_ln_b[i], od_out_w[i])
        x = x + g1 * rms_norm(y_lat, post_mix_g[l])
        f_lat = conv_ffn(rms_norm(x, pre_ffn_g[l]) * (1 + sc2) + sh2,
                         ffn_in_w[l], ffn_dw_w[l], ffn_dw_b[l], ffn_out_w[l])
        x = x + g2 * rms_norm(f_lat, post_ffn_g[l])
        if need_ctx:
            ctx = ctx + cg1 * rms_norm(y_ctx, post_mix_g[l])
            f_ctx = conv_ffn(rms_norm(ctx, pre_ffn_g[l]) * (1 + csc2) + csh2,
                             ffn_in_w[l], ffn_dw_w[l], ffn_dw_b[l], ffn_out_w[l])
            ctx = ctx + cg2 * rms_norm(f_ctx, post_ffn_g[l])
    return x
```

```python
import contextlib
import numpy as np
import ml_dtypes
import concourse.bass as bass
import concourse.mybir as mybir
from concourse.bass_utils import run_bass_kernel_spmd

F32 = mybir.dt.float32
BF16 = mybir.dt.bfloat16
AF = mybir.ActivationFunctionType
ALU = mybir.AluOpType

P = 128
D = 1024
KC = 8
SEQ = 2048
CTXL = 256
TT = SEQ + CTXL
DEPTH = 4
DFF = 2816
NFC = 22
EPS = 1e-6
MLA_SCALE = 96.0 ** -0.5
GRID_W = 64
ROPE_BASE = 10000.0


class Tok:
    __slots__ = ("w", "r", "excl")

    def __init__(self, excl=False):
        self.w = None
        self.r = {}
        self.excl = excl


class Sched:
    ENGS = ("pe", "act", "dve", "pool", "sp")

    def __init__(self, nc, es, n_dma_sems=40, same_engine_sync=True):
        self.nc = nc
        self.same = same_engine_sync
        self.ops = {e: [] for e in self.ENGS}
        self.cnt = {e: 0 for e in self.ENGS}
        self.sems = {}
        for e in self.ENGS:
            self.sems[e] = es.enter_context(nc.semaphore("s_" + e))
        self.dsems = []
        for i in range(n_dma_sems):
            k = "d%d" % i
            self.sems[k] = es.enter_context(nc.semaphore("s_" + k))
            self.dsems.append(k)
        self.dtot = {k: 0 for k in self.dsems}
        nsw = (2 * n_dma_sems) // 3
        self.dpool = {"sw": self.dsems[:nsw], "hw": self.dsems[nsw:]}
        self.drr = {"sw": 0, "hw": 0}
        self.known = {e: {} for e in self.ENGS}
        self.out_waits = []
        self.ninstr = 0

    def _wait(self, eng, k, v):
        kn = self.known[eng]
        if kn.get(k, 0) < v:
            kn[k] = v
            self.ops[eng].append(("wait", k, v))

    def _need(self, eng, reads, writes):
        need = {}

        def add(dep):
            if dep is None:
                return
            k, v = dep
            if k == eng and (eng == "pe" or not self.same):
                return
            if need.get(k, 0) < v:
                need[k] = v
        for t in reads:
            add(t.w)
        for t in writes:
            add(t.w)
            for k, v in t.r.items():
                add((k, v))
        for k, v in need.items():
            self._wait(eng, k, v)

    def op(self, eng, fn, reads=(), writes=()):
        ex = [t for t in reads if t.excl]
        if ex:
            writes = list(writes) + ex
            reads = [t for t in reads if not t.excl]
        self._need(eng, reads, writes)
        self.cnt[eng] += 1
        n = self.cnt[eng]
        self.ops[eng].append(("ins", fn, eng, 1))
        for t in reads:
            if t.r.get(eng, 0) < n:
                t.r[eng] = n
        for t in writes:
            t.w = (eng, n)
            t.r = {}
        self.ninstr += 1

    def dma(self, q, out, in_, reads=(), writes=(), is_output=False):
        pk = "sw" if q == "pool" else "hw"
        lst = self.dpool[pk]
        k = lst[self.drr[pk]]
        self.drr[pk] = (self.drr[pk] + 1) % len(lst)
        self._wait(q, k, self.dtot[k])
        self._need(q, reads, writes)
        self.dtot[k] += 16
        v = self.dtot[k]
        self.ops[q].append(("ins", lambda e, o=out, i=in_: e.dma_start(out=o, in_=i), k, 16))
        for t in reads:
            if t.r.get(k, 0) < v:
                t.r[k] = v
        for t in writes:
            t.w = (k, v)
            t.r = {}
        if is_output:
            self.out_waits.append((k, v))
        self.ninstr += 1

    def barrier(self):
        for e in self.ENGS:
            for f in self.ENGS:
                if f != e and self.cnt[f] > 0:
                    self._wait(e, f, self.cnt[f])
            for k in self.dsems:
                if self.dtot[k] > 0:
                    self._wait(e, k, self.dtot[k])

    def emit(self):
        nc = self.nc
        for k, v in self.out_waits:
            self._wait("sp", k, v)
        sems = self.sems

        def replay(e, name):
            for o in self.ops[name]:
                if o[0] == "wait":
                    e.wait_ge(sems[o[1]], o[2])
                else:
                    o[1](e).then_inc(sems[o[2]], o[3])
        with nc.Block() as block:
            @block.sync
            def _(e):
                replay(e, "sp")

            @block.tensor
            def _(e):
                replay(e, "pe")

            @block.scalar
            def _(e):
                replay(e, "act")

            @block.vector
            def _(e):
                replay(e, "dve")

            @block.gpsimd
            def _(e):
                replay(e, "pool")


class Buf:
    def __init__(self, t, shape, gran=256, excl=False):
        self.t = t
        self.excl = excl
        self.shape = shape
        self.gran = gran
        self.ncols = shape[-1]
        self.nk = shape[1] if len(shape) == 3 else 1
        self.toks = {}

    def _tok(self, kc, g):
        key = (kc, g)
        t = self.toks.get(key)
        if t is None:
            t = Tok(self.excl)
            self.toks[key] = t
        return t

    def tk(self, kc=None, c0=0, c1=None):
        if c1 is None:
            c1 = self.ncols
        if c1 <= c0:
            return []
        kcs = range(self.nk) if kc is None else ([kc] if isinstance(kc, int) else list(kc))
        gs = range(c0 // self.gran, (c1 - 1) // self.gran + 1)
        return [self._tok(k, g) for k in kcs for g in gs]


def cols_layout():
    off = {}
    n = 0

    def add(name, w):
        nonlocal n
        off[name] = n
        n += w
    for l in range(DEPTH):
        add("modb%d" % l, 96)
        add("pmg%d" % l, 8)
        add("qmg%d" % l, 8)
        add("pfg%d" % l, 8)
        add("qfg%d" % l, 8)
        add("fdw%d" % l, 66)
        add("fdb%d" % l, 22)
    for i in range(2):
        add("qn%d" % i, 2)
        add("kvn%d" % i, 1)
        add("on%d" % i, 1)
        add("cdw%d" % i, 186)
        add("cdb%d" % i, 6)
        add("lng%d" % i, 6)
        add("lnb%d" % i, 6)
    return off, n


COFF, NCOL = cols_layout()
BLOCKS_ALL = [(0, 256), (256, 768), (768, 1280), (1280, 1792), (1792, 2304)]


class KB:
    def __init__(self, n_layers=DEPTH, dumps=()):
        self.n_layers = n_layers
        self.nc = nc = bass.Bass("TRN2", target_bir_lowering=False)
        self.dumps = dumps
        BFN = ("dftN", "dftC", "dft64", "ropeT")
        dr = lambda name, shape, kind="ExternalInput": nc.dram_tensor(name, shape, BF16 if name in BFN else F32, kind=kind).ap()
        self.d = {}
        for name, shape in [
            ("xT", [D, SEQ]), ("ctxT", [D, CTXL]), ("ccol", [P, 16]), ("cols", [P, NCOL]),
            ("mod_w", [DEPTH, D, 6 * D]), ("ev_in_w", [2, D, 1984]), ("w_krp", [2, D, 96]),
            ("w_uq", [2, 256, 768]), ("w_uqp", [2, 256, 768]), ("w_ukv", [2, 128, 1024]),
            ("w_gate", [2, 64, 512]), ("ev_out_w", [2, KC, P, KC, P]), ("od_in_w", [2, 14, P, KC, P]),
            ("od_out_w", [2, KC, P, KC, P]), ("ffn_wgv", [DEPTH, NFC, P, KC, 256]), ("ffn_wo", [DEPTH, KC, P, NFC, P]),
            ("ev_gla", [2, 2, P, KC, 800]), ("ident", [P, P]), ("ropeT", [32, 2, SEQ]), ("masks", [P, 6, P]),
            ("dftN", [2, SEQ, SEQ]), ("dftC", [2, CTXL, CTXL]), ("dft64", [256, 512]),
        ]:
            self.d[name] = dr(name, shape)
        self.outT = dr("outT", [D, SEQ], kind="ExternalOutput")
        self.wgv_bf = nc.dram_tensor("wgv_bf", [DEPTH, NFC, P, KC, 256], BF16, kind="Internal").ap()
        self.wgv_tok = {}
        self.dump_aps = {}

    def sb(self, es, name, shape, dt, gran=256):
        self._nm = getattr(self, "_nm", 0) + 1
        t = es.enter_context(self.nc.sbuf_tensor("%s_%d" % (name, self._nm), shape, dt))
        return Buf(t, shape, gran)

    def E(self, eng, method, *args, r=(), w=(), **kw):
        self.S.op(eng, lambda e: getattr(e, method)(*args, **kw), reads=r, writes=w)

    def A(self, out, in_, func, r=(), w=(), **kw):
        self.S.op("act", lambda e: e.activation(out, in_, func, **kw), reads=r, writes=w)

    def mm(self, out, lhsT, rhs, start, stop, r=(), w=(), rg=(0, 128)):
        groups = frozenset(range(rg[0] // 32, (rg[0] + rg[1] + 31) // 32))
        S = self.S
        for t in w:
            prev = self._bank_rg.get(id(t))
            if prev is not None and prev[0] != groups:
                S._wait("pe", "pe", prev[1])
        S.op("pe", lambda e: e.matmul(out, lhsT, rhs, start=start, stop=stop), reads=r, writes=w)
        for t in w:
            self._bank_rg[id(t)] = (groups, S.cnt["pe"])

    def bank(self, pool):
        lst = {"a": (0, 1, 2, 3), "b": (4, 5), "c": (6, 7), "g": (0, 1, 5), "v": (2, 3, 4), "x": (5, 6)}[pool]
        i = self._bk.get(pool, 0)
        self._bk[pool] = i + 1
        b = lst[i % len(lst)]
        return b, self.PS.tk(b)

    def tmpf(self):
        i = self._tf
        self._tf += 1
        b = self.TF[i % len(self.TF)]
        return b.t, b.tk()

    def tmpr(self):
        i = self._tr
        self._tr += 1
        b = self.TR[i % len(self.TR)]
        return b.t, b.tk()

    def tmpb(self):
        i = self._tb
        self._tb += 1
        b = self.TB[i % len(self.TB)]
        return b.t, b.tk()

    def col(self, name, j0=0, w=1):
        o = COFF[name] + j0
        return self.COLS.t[:, o:o + w]

    def xsrc(self, kc, c0, c1):
        if c1 <= CTXL:
            return self.CT.t[:, kc, c0:c1], self.CT.tk(kc, c0, c1)
        assert c0 >= CTXL
        return self.X.t[:, kc, c0 - CTXL:c1 - CTXL], self.X.tk(kc, c0 - CTXL, c1 - CTXL)

    def modv(self, l, w, kind):
        return self.DER.t[:, (l * 2 + w) * 4 + kind, :]

    def modb(self, l, w, which):
        return self.MOD.t[:, l, which * 8:(which + 1) * 8, w]

    def convert_ffn_weights(self, l):
        for fc in range(NFC):
            t = Tok()
            self.wgv_tok[(l, fc)] = t
            self.S.dma("pool", self.wgv_bf[l, fc], self.d["ffn_wgv"][l, fc], writes=[t])

    def ckpt(self, name):
        import os
        if not hasattr(self, "marks"):
            self.marks = []
        self.marks.append((name, self.S.cnt["pe"]))
        if os.environ.get("KB_STOP") == name:
            print("stopped at", name)
            self.stop = True
        return self.stop

    def dump(self, name, ap, toks, shape):
        o = self.nc.dram_tensor("dbg_" + name, shape, F32, kind="ExternalOutput").ap()
        self.S.dma("pool", o, ap, reads=toks, is_output=True)

    def build(self):
        nc = self.nc
        with contextlib.ExitStack() as es:
            self.S = Sched(nc, es)
            self._bk = {}
            self._bank_rg = {}
            self._tf = 0
            self._tr = 0
            self._tb = 0
            pst = es.enter_context(nc.psum_tensor("ps", [P, 8, 512], F32))
            self.PS = Buf(pst, [P, 8, 512], gran=512, excl=True)
            self.X = self.sb(es, "X", [P, KC, SEQ], F32)
            self.CT = self.sb(es, "CT", [P, KC, CTXL], F32)
            self.COLS = self.sb(es, "COLS", [P, NCOL], F32, gran=1 << 20)
            self.MOD = self.sb(es, "MOD", [P, DEPTH, 48, 2], F32, gran=1 << 20)
            self.MOD.nk = 1
            self.DER = self.sb(es, "DER", [P, DEPTH * 2 * 4, 8], F32, gran=1 << 20)
            self.DER.nk = 1
            self.IDENT = self.sb(es, "IDENT", [P, P], BF16)
            self.ONES = self.sb(es, "ONES", [P, P], BF16)
            self.TF = [self.sb(es, "TF%d" % i, [P, 512], F32, gran=512) for i in range(4)]
            self.TB = [self.sb(es, "TB%d" % i, [P, 512], BF16, gran=512) for i in range(4)]
            self.TR = [self.sb(es, "TR%d" % i, [P, 512], F32, gran=512) for i in range(3)]
            S = self.S
            for kc in range(KC):
                S.dma("sp", self.X.t[:, kc, :], self.d["xT"][kc * P:(kc + 1) * P, :], writes=self.X.tk(kc))
            S.dma("sp", self.CT.t[:, :, :], self.d["ctxT"].rearrange("(k p) n -> p k n", p=P), writes=self.CT.tk())
            S.dma("sp", self.COLS.t[:, :], self.d["cols"], writes=self.COLS.tk())
            S.dma("pool", self.IDENT.t[:, :], self.d["ident"], writes=self.IDENT.tk())
            self.E("pool", "memset", self.ONES.t[:, :], 1.0, w=self.ONES.tk())
            self.CST = self.sb(es, "CST", [P, 4], F32, gran=1 << 20)
            self.E("pool", "memset", self.CST.t[:, 0:1], 1.0, w=self.CST.tk())
            self.E("pool", "memset", self.CST.t[:, 1:2], 1.0 / 768, w=self.CST.tk())
            self.E("pool", "memset", self.CST.t[:, 2:3], 0.0, w=self.CST.tk())
            import os
            if os.environ.get('KB_SKIP_MOD') != '1':
                self.modulation(es)
            self.stop = False
            for l in range(self.n_layers):
                need_ctx = l < 2
                if self.stop:
                    break
                if l % 2 == 0:
                    self.even_mixer(l, need_ctx)
                else:
                    self.odd_mixer(l, need_ctx)
                if self.stop or self.ckpt("mixer%d" % l):
                    break
                self.ffn(l, need_ctx)
                if self.ckpt("ffn%d" % l):
                    break
            S.barrier()
            for kc in range(KC):
                S.dma("sp", self.outT[kc * P:(kc + 1) * P, :], self.X.t[:, kc, :], reads=self.X.tk(kc), is_output=True)
            if "ctx" in self.dumps:
                self.dump("ctx", self.CT.t[:, :, :], self.CT.tk(),
                          [P, KC, CTXL])
            S.emit()
        return nc

    def modulation(self, es0):
        S = self.S
        SC = self.sb(es0, "SC", [P, KC, 2], BF16, gran=1 << 20)
        SC.nk = 1
        self.SC = SC
        with contextlib.ExitStack() as es:
            CC = self.sb(es, "CC", [P, 16], F32, gran=1 << 20)
            WM = [self.sb(es, "WM%d" % i, [P, KC, 512], BF16, gran=512) for i in range(3)]
            S.dma("sp", CC.t[:, :], self.d["ccol"], writes=CC.tk())
            self.A(SC.t[:, :, :], CC.t[:, :], AF.Silu, r=CC.tk(), w=SC.tk())
            nw = 0
            import os
            stage = int(os.environ.get('KB_MOD_STAGE', '9'))
            for l in range(1 if stage >= 2 else 0):
                bank, btk = self.bank("b")
                for q in range(12):
                    wm = WM[nw % 3]
                    nw += 1
                    S.dma("pool", wm.t[:, :, :],
                          self.d["mod_w"][l, :, q * 512:(q + 1) * 512].rearrange("(k p) n -> p k n", p=P),
                          writes=wm.tk())
                    for jj in range(4):
                        j = q * 4 + jj
                        for kc in range(KC):
                            self.mm(self.PS.t[:, bank, 2 * j:2 * j + 2], wm.t[:, kc, jj * P:(jj + 1) * P],
                                    SC.t[:, kc, :], kc == 0, kc == KC - 1, r=wm.tk() + SC.tk(), w=btk)
                if stage < 3:
                    continue
                self.mod_final(l, bank, btk)
            S.barrier()

    def mod_final(self, l, bank, btk):
        self.E("dve", "tensor_tensor", self.MOD.t[:, l, :, :].rearrange("p a b -> p (a b)"), self.PS.t[:, bank, 0:96],
               self.col("modb%d" % l, 0, 96), ALU.add, r=btk + self.COLS.tk(), w=self.MOD.tk())
        for w in range(2):
            for kind, (mi, gname, plus1) in enumerate([(1, "pmg", True), (2, "qmg", False),
                                                       (4, "pfg", True), (5, "qfg", False)]):
                src = self.MOD.t[:, l, mi * 8:(mi + 1) * 8, w]
                g = self.col("%s%d" % (gname, l), 0, 8)
                dst = self.modv(l, w, kind)
                if plus1:
                    self.E("dve", "scalar_tensor_tensor", dst, src, self.CST.t[:, 0:1], g, ALU.add, ALU.mult,
                           r=self.MOD.tk() + self.COLS.tk() + self.CST.tk(), w=self.DER.tk())
                else:
                    self.E("dve", "tensor_tensor", dst, src, g, ALU.mult,
                           r=self.MOD.tk() + self.COLS.tk(), w=self.DER.tk())

    def mod_tasks(self, l, wm):
        S = self.S
        SC = self.SC
        bank = 7
        btk = self.PS.tk(bank)
        sched = {}
        for q in range(12):
            def dma(q=q):
                S.dma("pool", wm.t[:, :, :],
                      self.d["mod_w"][l, :, q * 512:(q + 1) * 512].rearrange("(k p) n -> p k n", p=P), writes=wm.tk())

            def mms(q=q):
                for jj in range(4):
                    j = q * 4 + jj
                    for kc in range(KC):
                        self.mm(self.PS.t[:, bank, 2 * j:2 * j + 2], wm.t[:, kc, jj * P:(jj + 1) * P],
                                SC.t[:, kc, :], kc == 0, kc == KC - 1, r=wm.tk() + SC.tk(), w=btk)
            sched[2 + 5 * q] = dma
            sched[6 + 5 * q] = mms
        sched[6 + 5 * 11 + 2] = lambda: self.mod_final(l, bank, btk)
        return sched

    def stats(self, srcfn, n, nchunks, dim, out_ap, out_tk, rows=(0, P)):
        bank, btk = self.bank("b")
        for kc in range(nchunks):
            src, stk = srcfn(kc)
            sq, sqtk = self.tmpb()
            self.A(sq[:, :n], src, AF.Square, r=stk, w=sqtk)
            self.mm(self.PS.t[:, bank, :n], self.ONES.t[:, :], sq[:, :n], kc == 0, kc == nchunks - 1,
                    r=sqtk + self.ONES.tk(), w=btk)
        self.rstd_from_ps(bank, btk, n, dim, out_ap, out_tk, rows)

    def rstd_from_ps(self, bank, btk, n, dim, out_ap, out_tk, rows=(0, P)):
        tmp, ttk = self.tmpf()
        r0, r1 = rows
        self.A(tmp[r0:r1, :n], self.PS.t[r0:r1, bank, :n], AF.Ln, bias=EPS, scale=1.0 / dim,
               r=btk, w=ttk)
        self.A(out_ap, tmp[r0:r1, :n], AF.Exp, scale=-0.5, r=ttk, w=out_tk)

    def hblock(self, l, c0, c1, kind_a, which_b, RS, rc0, HB, o0):
        n = c1 - c0
        if RS is None:
            rs, rtk = self.tmpr()
            RS = Buf(rs, [P, 512], 512)
            RS.toks[(0, 0)] = rtk[0]
            rc0 = 0
            self.stats(lambda kc: self.xsrc(kc, c0, c1), n, KC, D, rs[:, :n], rtk)
        w = 1 if c1 <= CTXL else 0
        A_ = self.modv(l, w, kind_a)
        B_ = self.modb(l, w, which_b)
        for kc in range(KC):
            src, stk = self.xsrc(kc, c0, c1)
            tmp, ttk = self.tmpf()
            self.E("dve", "tensor_tensor", tmp[:, :n], src, RS.t[:, rc0:rc0 + n], ALU.mult,
                   r=stk + RS.tk(0, rc0, rc0 + n), w=ttk)
            self.E("dve", "tensor_scalar", HB.t[:, kc, o0:o0 + n], tmp[:, :n], A_[:, kc:kc + 1], B_[:, kc:kc + 1],
                   ALU.mult, ALU.add, r=ttk + self.DER.tk() + self.MOD.tk(), w=HB.tk(kc, o0, o0 + n))

    def post_residual(self, l, YB, y0, c0, c1, kind_g):
        n = c1 - c0
        w = 1 if c1 <= CTXL else 0
        G_ = self.modv(l, w, kind_g)
        rs, rtk = self.tmpr()
        self.stats(lambda kc: (YB.t[:, kc, y0:y0 + n], YB.tk(kc, y0, y0 + n)), n, KC, D, rs[:, :n], rtk)
        for kc in range(KC):
            xs, xtk = self.xsrc(kc, c0, c1)
            tmp, ttk = self.tmpf()
            self.E("dve", "tensor_tensor", tmp[:, :n], YB.t[:, kc, y0:y0 + n], rs[:, :n], ALU.mult,
                   r=YB.tk(kc, y0, y0 + n) + rtk, w=ttk)
            self.E("dve", "scalar_tensor_tensor", xs, tmp[:, :n], G_[:, kc:kc + 1], xs, ALU.mult, ALU.add,
                   r=ttk + xtk + self.DER.tk(), w=xtk)

    def out_proj(self, l, wname, i, MO, blocks, es_outer):
        S = self.S
        with contextlib.ExitStack() as es:
            WO = self.sb(es, "WO", [P, KC, D], BF16, gran=P)
            YB = self.sb(es, "YB", [P, KC, 512], F32, gran=512)
            for oc in range(KC):
                S.dma("pool", WO.t[:, :, oc * P:(oc + 1) * P], self.d[wname][i, oc], writes=WO.tk(None, oc * P, (oc + 1) * P))
            for (c0, c1) in blocks:
                n = c1 - c0
                for oc in range(KC):
                    bank, btk = self.bank("a")
                    for kc in range(KC):
                        self.mm(self.PS.t[:, bank, :n], WO.t[:, kc, oc * P:(oc + 1) * P], MO.t[:, kc, c0:c1],
                                kc == 0, kc == KC - 1, r=WO.tk(kc, oc * P, (oc + 1) * P) + MO.tk(kc, c0, c1), w=btk)
                    self.A(YB.t[:, oc, :n], self.PS.t[:, bank, :n], AF.Identity, r=btk, w=YB.tk(oc, 0, n))
                self.post_residual(l, YB, 0, c0, c1, 1)
            S.barrier()

    def ffn(self, l, need_ctx):
        S = self.S
        lat = [(CTXL + t0, min(410, SEQ - t0), CTXL, TT) for t0 in range(0, SEQ, 410)]
        ctxs = [(0, CTXL, 0, CTXL)]
        if need_ctx:
            groups = [ctxs + lat[0:1], lat[1:3], lat[3:5]]
        else:
            groups = [lat[0:2], lat[2:4], lat[4:5]]
        with contextlib.ExitStack() as es:
            HF = self.sb(es, "HF", [P, KC, 832], BF16)
            U = self.sb(es, "U", [P, NFC, 820], BF16, gran=1 << 20)
            YB = self.sb(es, "YBF", [P, KC, 820], F32, gran=1 << 20)
            WG = [self.sb(es, "WG%d" % i, [P, KC, 256], BF16, gran=1 << 20) for i in range(2)]
            WOt = [self.sb(es, "WOt%d" % i, [P, NFC, P], BF16, gran=1 << 20) for i in range(2)]
            DG = [self.sb(es, "DG%d" % i, [P, 3, P], BF16, gran=1 << 20) for i in range(3)]
            GSB = [self.sb(es, "GSB%d" % i, [P, 512], BF16, gran=512) for i in range(3)]
            nwg = nwo = ngs = ndg = 0
            msched = {}
            if l + 1 < self.n_layers:
                WMf = self.sb(es, "WMf", [P, KC, 512], BF16, gran=512)
                msched = self.mod_tasks(l + 1, WMf)
            mstep = 0
            HL = self.sb(es, "HL", [P, KC, 8], BF16, gran=1 << 20)
            for gi, grp in enumerate(groups):
                c0, n, lo, hi = grp[0]
                if gi > 0 and c0 - 1 >= lo:
                    self.hblock(l, c0 - 1, c0, 2, 3, None, 0, HL, gi)
            for gi, grp in enumerate(groups):
                hoff = []
                ho = 0
                for si, (c0, n, lo, hi) in enumerate(grp):
                    e0 = max(lo, c0 - 1)
                    e1 = min(hi, c0 + n + 1)
                    self.hblock(l, e0, e1, 2, 3, None, 0, HF, ho)
                    if si == 0 and gi > 0 and c0 - 1 >= lo:
                        self.E("dve", "tensor_copy", HF.t[:, :, ho:ho + 1], HL.t[:, :, gi:gi + 1],
                               r=HL.tk(), w=HF.tk(None, ho, ho + 1))
                    hoff.append((ho, e0, e1))
                    ho += e1 - e0
                self.ckpt("f_h")
                uoffs = []
                uo = 0
                for (c0, n, lo, hi) in grp:
                    uoffs.append(uo)
                    uo += n
                fcbuf = {}
                dgbuf = {}

                def wload(fc):
                    nonlocal nwg
                    if fc >= NFC or fc in fcbuf:
                        return
                    wg = WG[nwg % 2]
                    nwg += 1
                    fcbuf[fc] = wg
                    S.dma("sp", wg.t[:, :, :], self.wgv_bf[l, fc], reads=[self.wgv_tok[(l, fc)]], writes=wg.tk())

                def dbuild(fc):
                    nonlocal ndg
                    if fc >= NFC or fc in dgbuf:
                        return
                    dg = DG[ndg % 3]
                    ndg += 1
                    dgbuf[fc] = dg
                    for k in range(3):
                        self.E("dve", "tensor_scalar", dg.t[:, k, :], self.IDENT.t[:, :],
                               self.col("fdw%d" % l, fc * 3 + k, 1), self.CST.t[:, 2:3], ALU.mult, ALU.add,
                               r=self.IDENT.tk() + self.COLS.tk(), w=dg.tk())

                def prep(fc, si):
                    wload(fc)
                    wload(fc + 1)
                    wg = fcbuf[fc]
                    c0, n, lo, hi = grp[si]
                    ho, e0, e1 = hoff[si]
                    ne = e1 - e0
                    bank, btk = self.bank("g")
                    for kc in range(KC):
                        self.mm(self.PS.t[:, bank, :ne], wg.t[:, kc, 0:P], HF.t[:, kc, ho:ho + ne],
                                kc == 0, kc == KC - 1, r=wg.tk() + HF.tk(kc, ho, ho + ne), w=btk)
                    bank3, btk3 = self.bank("v")
                    vo = ho + (c0 - e0)
                    for kc in range(KC):
                        self.mm(self.PS.t[:, bank3, :n], wg.t[:, kc, P:2 * P], HF.t[:, kc, vo:vo + n],
                                kc == 0, kc == KC - 1, r=wg.tk() + HF.tk(kc, vo, vo + n), w=btk3)
                    return [bank, btk, bank3, btk3]

                def conv(fc, si, st):
                    bank, btk, bank3, btk3 = st
                    c0, n, lo, hi = grp[si]
                    ho, e0, e1 = hoff[si]
                    off = c0 - e0
                    cbuf, ctk = self.tmpf()
                    self.A(cbuf[:, :n], self.PS.t[:, bank, off:off + n], AF.Identity,
                           bias=self.col("fdb%d" % l, fc, 1), scale=self.col("fdw%d" % l, fc * 3 + 1, 1),
                           r=btk + self.COLS.tk(), w=ctk)
                    a = 1 if off == 0 else 0
                    self.E("dve", "scalar_tensor_tensor", cbuf[:, a:n], self.PS.t[:, bank, off - 1 + a:off - 1 + n],
                           self.col("fdw%d" % l, fc * 3 + 0, 1), cbuf[:, a:n], ALU.mult, ALU.add,
                           r=btk + ctk + self.COLS.tk(), w=ctk)
                    b = n - 1 if e1 == c0 + n else n
                    self.E("dve", "scalar_tensor_tensor", cbuf[:, 0:b], self.PS.t[:, bank, off + 1:off + 1 + b],
                           self.col("fdw%d" % l, fc * 3 + 2, 1), cbuf[:, 0:b], ALU.mult, ALU.add,
                           r=btk + ctk + self.COLS.tk(), w=ctk)
                    st += [cbuf, ctk]

                def fin(fc, si, st):
                    bank, btk, bank3, btk3, cbuf, ctk = st
                    c0, n, lo, hi = grp[si]
                    uo = uoffs[si]
                    ge, getk = self.tmpb()
                    self.A(ge[:, :n], cbuf[:, :n], AF.Gelu, r=ctk, w=getk)
                    self.E("dve", "tensor_tensor", U.t[:, fc, uo:uo + n], self.PS.t[:, bank3, :n], ge[:, :n],
                           ALU.mult, r=getk + btk3, w=U.tk(fc))
                items = [(fc, si) for fc in range(NFC) for si in range(len(grp))]
                NI = len(items)
                sts = {}
                for k in range(NI + 2):
                    if k < NI:
                        if mstep in msched:
                            msched.pop(mstep)()
                        mstep += 1
                        sts[k] = prep(*items[k])
                    if 0 <= k - 1 < NI:
                        conv(items[k - 1][0], items[k - 1][1], sts[k - 1])
                    if 0 <= k - 2 < NI:
                        fin(items[k - 2][0], items[k - 2][1], sts[k - 2])
                        del sts[k - 2]
                self.ckpt("f_main")
                for oc in range(KC):
                    wo = WOt[nwo % 2]
                    nwo += 1
                    S.dma("pool", wo.t[:, :, :], self.d["ffn_wo"][l, oc], writes=wo.tk())
                    uo = 0
                    for (c0, n, lo, hi) in grp:
                        bank, btk = self.bank("a")
                        for fc in range(NFC):
                            self.mm(self.PS.t[:, bank, :n], wo.t[:, fc, :], U.t[:, fc, uo:uo + n],
                                    fc == 0, fc == NFC - 1, r=wo.tk() + U.tk(fc), w=btk)
                        self.A(YB.t[:, oc, uo:uo + n], self.PS.t[:, bank, :n], AF.Identity, r=btk, w=YB.tk(oc))
                        uo += n
                self.ckpt("f_out")
                uo = 0
                for (c0, n, lo, hi) in grp:
                    self.post_residual(l, YB, uo, c0, c0 + n, 3)
                    uo += n
            for kk in sorted(msched):
                msched.pop(kk)()
            S.barrier()

    def odd_mixer(self, l, need_ctx):
        S = self.S
        i = l // 2
        blocks = BLOCKS_ALL if need_ctx else BLOCKS_ALL[1:]
        PADW = 15
        with contextlib.ExitStack() as es:
            MO = self.sb(es, "MOo", [P, KC, TT], BF16)
            UC = self.sb(es, "UC", [P, 6, SEQ + 2 * PADW], BF16)
            UCC = self.sb(es, "UCC", [P, 6, CTXL + 2 * PADW], BF16, gran=1 << 20)
            for c in range(6):
                self.E("pool", "memset", UC.t[:, c, 0:PADW], 0.0, w=UC.tk(c, 0, PADW))
                self.E("pool", "memset", UC.t[:, c, SEQ + PADW:SEQ + 2 * PADW], 0.0,
                       w=UC.tk(c, SEQ + PADW, SEQ + 2 * PADW))
                if need_ctx:
                    self.E("pool", "memset", UCC.t[:, c, 0:PADW], 0.0, w=UCC.tk(c))
                    self.E("pool", "memset", UCC.t[:, c, CTXL + PADW:CTXL + 2 * PADW], 0.0, w=UCC.tk(c))
            with contextlib.ExitStack() as es1:
                FT = self.sb(es1, "FT", [P, 18, 256], BF16, gran=1 << 20)
                with contextlib.ExitStack() as es2:
                    HB = self.sb(es2, "HB", [P, KC, 512], BF16, gran=512)
                    WI = self.sb(es2, "WI", [P, KC, 1792], BF16, gran=P)
                    for j_ in [0, 6, 1, 7, 2, 8, 3, 9, 4, 10, 5, 11, 12, 13]:
                        S.dma("pool", WI.t[:, :, j_ * P:(j_ + 1) * P], self.d["od_in_w"][i, j_],
                              writes=WI.tk(None, j_ * P, (j_ + 1) * P))
                    self.convert_ffn_weights(l)
                    for (c0, c1) in blocks:
                        n = c1 - c0
                        isctx = c1 <= CTXL
                        self.hblock(l, c0, c1, 0, 0, None, 0, HB, 0)
                        for c in range(6):
                            ba, bta = self.bank("a")
                            for kc in range(KC):
                                self.mm(self.PS.t[:, ba, :n], WI.t[:, kc, c * P:(c + 1) * P], HB.t[:, kc, :n],
                                        kc == 0, kc == KC - 1, r=WI.tk(kc, c * P, (c + 1) * P) + HB.tk(kc), w=bta)
                            bg, btg = self.bank("a")
                            for kc in range(KC):
                                self.mm(self.PS.t[:, bg, :n], WI.t[:, kc, 768 + c * P:768 + (c + 1) * P], HB.t[:, kc, :n],
                                        kc == 0, kc == KC - 1, r=WI.tk(kc, 768 + c * P, 768 + (c + 1) * P) + HB.tk(kc), w=btg)
                            sg, sgtk = self.tmpf()
                            self.A(sg[:, :n], self.PS.t[:, bg, :n], AF.Sigmoid, r=btg, w=sgtk)
                            if isctx:
                                dst, dtk = UCC.t[:, c, PADW + c0:PADW + c1], UCC.tk(c)
                            else:
                                dst, dtk = UC.t[:, c, PADW + c0 - CTXL:PADW + c1 - CTXL], UC.tk(c, PADW + c0 - CTXL, PADW + c1 - CTXL)
                            self.E("dve", "tensor_tensor", dst, self.PS.t[:, ba, :n], sg[:, :n], ALU.mult,
                                   r=bta + sgtk, w=dtk)
                        for t in range(n // P):
                            bf, btf = self.bank("a")
                            for kc in range(KC):
                                self.mm(self.PS.t[:, bf, 0:256], HB.t[:, kc, t * P:(t + 1) * P], WI.t[:, kc, 1536:1792],
                                        kc == 0, kc == KC - 1, r=WI.tk(kc, 1536, 1792) + HB.tk(kc), w=btf)
                            tt = c0 // P + t
                            self.A(FT.t[:, tt, :], self.PS.t[:, bf, 0:256], AF.Identity, r=btf, w=FT.tk(tt))
                    S.barrier()
                self.ckpt("od_ip")
                with contextlib.ExitStack() as es2:
                    DN = [self.sb(es2, "DN%d" % k, [P, 2, 4, 512], BF16, gran=1 << 20) for k in range(3)]
                    for b in DN:
                        b.nk = 1
                    D64 = self.sb(es2, "D64", [P, 2, 512], BF16, gran=1 << 20)
                    D64.nk = 1
                    ZB = self.sb(es2, "ZB", [P, 4, 512], BF16, gran=512)
                    S.dma("sp", D64.t[:, :, :], self.d["dft64"].rearrange("(k p) n -> p k n", p=P), writes=D64.tk())
                    ndn = 0
                    streams = [("dftN", SEQ, CTXL, 16)]
                    if need_ctx:
                        streams = [("dftC", CTXL, 0, 2)] + streams
                    for (dname, L, tbase, ntile) in streams:
                        for nb in range(0, L, 512):
                            n = min(512, L - nb)
                            banks = [self.bank("a") for _ in range(4)]
                            for m0 in range(0, ntile, 4):
                                mt = min(4, ntile - m0)
                                dn = DN[ndn % 3]
                                ndn += 1
                                for cs in range(2):
                                    S.dma("sp", dn.t[:, cs, 0:mt, :n],
                                          self.d[dname][cs, m0 * P:(m0 + mt) * P, nb:nb + n].rearrange("(m p) n -> p m n", p=P),
                                          writes=dn.tk())
                                for cs in range(2):
                                    for ch in range(2):
                                        bk, btk = banks[cs * 2 + ch]
                                        for m in range(mt):
                                            tt = tbase // P + m0 + m
                                            self.mm(self.PS.t[:, bk, :n], FT.t[:, tt, ch * P:(ch + 1) * P], dn.t[:, cs, m, :n],
                                                    (m0 + m) == 0, (m0 + m) == ntile - 1, r=FT.tk(tt) + dn.tk(), w=btk)
                            for cs in range(2):
                                for ch in range(2):
                                    bk, btk = banks[cs * 2 + ch]
                                    self.A(ZB.t[:, cs * 2 + ch, :n], self.PS.t[:, bk, :n], AF.Identity, r=btk, w=ZB.tk(cs * 2 + ch))
                            for ch in range(2):
                                bo, bto = self.bank("b")
                                for cs in range(2):
                                    self.mm(self.PS.t[:, bo, :n], D64.t[:, ch, cs * 256 + ch * P:cs * 256 + (ch + 1) * P],
                                            ZB.t[:, cs * 2 + ch, :n], cs == 0, cs == 1,
                                            r=D64.tk() + ZB.tk(cs * 2 + ch), w=bto)
                                self.A(MO.t[:, 6 + ch, tbase + nb:tbase + nb + n], self.PS.t[:, bo, :n], AF.Identity,
                                       r=bto, w=MO.tk(6 + ch, tbase + nb, tbase + nb + n))
                    S.barrier()
            self.ckpt("od_fn")
            with contextlib.ExitStack() as es1:
                CV = self.sb(es1, "CV", [P, 6, TT], BF16)
                DG = [self.sb(es1, "DGo%d" % k, [P, 31, P], BF16, gran=1 << 20) for k in range(2)]
                for b in DG:
                    b.nk = 1
                for c in range(6):
                    dg = DG[c % 2]
                    for k in range(31):
                        self.E("dve", "tensor_scalar", dg.t[:, k, :], self.IDENT.t[:, :],
                               self.col("cdw%d" % i, c * 31 + k, 1), self.CST.t[:, 2:3], ALU.mult, ALU.add,
                               r=self.IDENT.tk() + self.COLS.tk(), w=dg.tk())
                    for (c0, c1) in blocks:
                        n = c1 - c0
                        isctx = c1 <= CTXL
                        bk, btk = self.bank("a")
                        for k in range(31):
                            if isctx:
                                rhs, rtk = UCC.t[:, c, c0 + k:c0 + k + n], UCC.tk(c)
                            else:
                                a0 = c0 - CTXL + k
                                rhs, rtk = UC.t[:, c, a0:a0 + n], UC.tk(c, a0, a0 + n)
                            self.mm(self.PS.t[:, bk, :n], dg.t[:, k, :], rhs, k == 0, k == 30, r=dg.tk() + rtk, w=btk)
                        self.A(CV.t[:, c, c0:c1], self.PS.t[:, bk, :n], AF.Identity, bias=self.col("cdb%d" % i, c, 1),
                               r=btk + self.COLS.tk(), w=CV.tk(c, c0, c1))
                self.ckpt("od_cv")
                for (c0, c1) in blocks:
                    n = c1 - c0
                    b1, bt1 = self.bank("b")
                    b2, bt2 = self.bank("b")
                    for c in range(6):
                        self.mm(self.PS.t[:, b1, :n], self.ONES.t[:, :], CV.t[:, c, c0:c1], c == 0, c == 5,
                                r=CV.tk(c, c0, c1) + self.ONES.tk(), w=bt1)
                    for c in range(6):
                        sq, sqtk = self.tmpb()
                        self.A(sq[:, :n], CV.t[:, c, c0:c1], AF.Square, r=CV.tk(c, c0, c1), w=sqtk)
                        self.mm(self.PS.t[:, b2, :n], self.ONES.t[:, :], sq[:, :n], c == 0, c == 5,
                                r=sqtk + self.ONES.tk(), w=bt2)
                    mu, mutk = self.tmpr()
                    self.A(mu[:, :n], self.PS.t[:, b1, :n], AF.Identity, scale=1.0 / 768, r=bt1, w=mutk)
                    var, vtk = self.tmpf()
                    self.E("dve", "tensor_tensor", var[:, :n], mu[:, :n], mu[:, :n], ALU.mult, r=mutk, w=vtk)
                    self.E("dve", "scalar_tensor_tensor", var[:, :n], self.PS.t[:, b2, :n], self.CST.t[:, 1:2], var[:, :n],
                           ALU.mult, ALU.subtract, r=bt2 + vtk + self.CST.tk(), w=vtk)
                    rs, rtk = self.tmpr()
                    self.A(var[:, :n], var[:, :n], AF.Ln, bias=EPS, scale=1.0, r=vtk, w=vtk)
                    self.A(rs[:, :n], var[:, :n], AF.Exp, scale=-0.5, r=vtk, w=rtk)
                    for c in range(6):
                        t1, t1k = self.tmpf()
                        self.E("dve", "tensor_tensor", t1[:, :n], CV.t[:, c, c0:c1], mu[:, :n], ALU.subtract,
                               r=CV.tk(c, c0, c1) + mutk, w=t1k)
                        self.E("dve", "tensor_tensor", t1[:, :n], t1[:, :n], rs[:, :n], ALU.mult, r=t1k + rtk, w=t1k)
                        self.E("dve", "tensor_scalar", t1[:, :n], t1[:, :n], self.col("lng%d" % i, c, 1),
                               self.col("lnb%d" % i, c, 1), ALU.mult, ALU.add, r=t1k + self.COLS.tk(), w=t1k)
                        self.A(MO.t[:, c, c0:c1], t1[:, :n], AF.Silu, r=t1k, w=MO.tk(c, c0, c1))
                S.barrier()
            self.ckpt("od_ln")
            self.out_proj(l, "od_out_w", i, MO, blocks, es)

    def even_mixer(self, l, need_ctx):
        S = self.S
        i = l // 2
        with contextlib.ExitStack() as es:
            MO = self.sb(es, "MOe", [P, KC, TT], BF16)
            RS = None
            self.mla(l, i, need_ctx, MO, RS)
            if self.stop or self.ckpt("mla"):
                if "mo%d" % l in self.dumps:
                    self.dump("mo%d" % l, MO.t[:, 0:4, :], MO.tk(), [P, 4, TT])
                return
            for jp in range(2):
                self.gla_pair(l, i, jp, need_ctx, MO, RS)
                if self.stop or self.ckpt("gla%d" % jp):
                    return
            blocks = BLOCKS_ALL if need_ctx else BLOCKS_ALL[1:]
            if "mo%d" % l in self.dumps:
                self.dump("mo%d" % l, MO.t[:, :, :], MO.tk(), [P, KC, TT])
            self.out_proj(l, "ev_out_w", i, MO, blocks, es)

    def mla(self, l, i, need_ctx, MO, RS):
        S = self.S
        qblocks = BLOCKS_ALL if need_ctx else BLOCKS_ALL[1:]
        with contextlib.ExitStack() as es:
            QC = self.sb(es, "QC", [P, 2, TT], BF16)
            KVC = self.sb(es, "KVC", [P, TT], BF16)
            KR = self.sb(es, "KR", [P, TT], BF16)
            RSQ = self.sb(es, "RSQ", [P, TT], BF16)
            RSKV = self.sb(es, "RSKV", [P, TT], BF16)
            RSKVT = self.sb(es, "RSKVT", [P, 18], F32, gran=1 << 20)
            ROPE = self.sb(es, "ROPE", [P, 2, SEQ], BF16)
            S.dma("sp", ROPE.t[64:96, :, :], self.d["ropeT"], writes=ROPE.tk())
            with contextlib.ExitStack() as es1:
                HBs = [self.sb(es1, "HBm%d" % k, [P, KC, 512], BF16, gran=512) for k in range(2)]
                WA = self.sb(es1, "WA", [P, KC, 416], BF16, gran=1 << 20)
                WP = self.sb(es1, "WPk", [P, KC, 96], BF16, gran=1 << 20)
                S.dma("pool", WA.t[:, :, :], self.d["ev_in_w"][i, :, 0:416].rearrange("(k p) n -> p k n", p=P), writes=WA.tk())
                S.dma("pool", WP.t[:, :, :], self.d["w_krp"][i].rearrange("(k p) n -> p k n", p=P), writes=WP.tk())
                import os
                ipst = int(os.environ.get("KB_IP", "99"))
                self.hblock(l, BLOCKS_ALL[0][0], BLOCKS_ALL[0][1], 0, 0, None, 0, HBs[0], 0)
                for bi_, (c0, c1) in enumerate(BLOCKS_ALL):
                    HB = HBs[bi_ % 2]
                    n = c1 - c0
                    isctx = c1 <= CTXL
                    if bi_ + 1 < len(BLOCKS_ALL):
                        self.hblock(l, BLOCKS_ALL[bi_ + 1][0], BLOCKS_ALL[bi_ + 1][1], 0, 0, None, 0, HBs[(bi_ + 1) % 2], 0)

                    def proj(col0, ncol, wbuf=WA):
                        bk, btk = self.bank("a")
                        for kc in range(KC):
                            self.mm(self.PS.t[0:ncol, bk, :n], wbuf.t[:, kc, col0:col0 + ncol], HB.t[:, kc, :n],
                                    kc == 0, kc == KC - 1, r=wbuf.tk() + HB.tk(kc), w=btk)
                        return bk, btk
                    if need_ctx or not isctx:
                        bs, bts = self.bank("b")
                        for j in range(2):
                            bk, btk = proj(j * P, P)
                            self.E("dve", "tensor_scalar", QC.t[:, j, c0:c1], self.PS.t[:, bk, :n],
                                   self.col("qn%d" % i, j, 1), self.CST.t[:, 2:3], ALU.mult, ALU.add, r=btk + self.COLS.tk(), w=QC.tk(j, c0, c1))
                            sq, sqtk = self.tmpb()
                            self.A(sq[:, :n], self.PS.t[:, bk, :n], AF.Square, r=btk, w=sqtk)
                            self.mm(self.PS.t[:, bs, :n], self.ONES.t[:, :], sq[:, :n], j == 0, j == 1,
                                    r=sqtk + self.ONES.tk(), w=bts)
                        self.rstd_from_ps(bs, bts, n, 256, RSQ.t[:, c0:c1], RSQ.tk(0, c0, c1))
                    bk, btk = proj(256, P)
                    self.E("dve", "tensor_scalar", KVC.t[:, c0:c1], self.PS.t[:, bk, :n], self.col("kvn%d" % i, 0, 1),
                           self.CST.t[:, 2:3], ALU.mult, ALU.add, r=btk + self.COLS.tk(), w=KVC.tk(0, c0, c1))
                    sq, sqtk = self.tmpb()
                    self.A(sq[:, :n], self.PS.t[:, bk, :n], AF.Square, r=btk, w=sqtk)
                    bs, bts = self.bank("b")
                    self.mm(self.PS.t[:, bs, :n], self.ONES.t[:, :], sq[:, :n], True, True, r=sqtk + self.ONES.tk(), w=bts)
                    self.rstd_from_ps(bs, bts, n, 128, RSKV.t[:, c0:c1], RSKV.tk(0, c0, c1))
                    bs2, bts2 = self.bank("b")
                    for t in range(n // P):
                        self.mm(self.PS.t[:, bs2, t:t + 1], sq[:, t * P:(t + 1) * P], self.ONES.t[:, 0:1], True, True,
                                r=sqtk + self.ONES.tk(), w=bts2)
                    self.rstd_from_ps(bs2, bts2, n // P, 128, RSKVT.t[:, c0 // P:c0 // P + n // P], RSKVT.tk())
                    bk, btk = proj(320, 96)
                    if isctx:
                        self.E("dve", "tensor_copy", KR.t[64:96, c0:c1], self.PS.t[64:96, bk, :n], r=btk, w=KR.tk(0, c0, c1))
                    else:
                        bk2, btk2 = proj(0, 96, WP)
                        t1, t1k = self.tmpf()
                        t2, t2k = self.tmpf()
                        a0 = c0 - CTXL
                        self.E("dve", "tensor_tensor", t1[64:96, :n], self.PS.t[64:96, bk, :n], ROPE.t[64:96, 0, a0:a0 + n],
                               ALU.mult, r=btk + ROPE.tk(), w=t1k)
                        self.E("dve", "tensor_tensor", t2[64:96, :n], self.PS.t[64:96, bk2, :n], ROPE.t[64:96, 1, a0:a0 + n],
                               ALU.mult, r=btk2 + ROPE.tk(), w=t2k)
                        self.E("dve", "tensor_tensor", KR.t[64:96, c0:c1], t1[64:96, :n], t2[64:96, :n], ALU.add,
                               r=t1k + t2k, w=KR.tk(0, c0, c1))
                if "mlaip" in self.dumps:
                    self.dump("mod", self.MOD.t[:, :, :, :].rearrange("p l a b -> p (l a b)"), self.MOD.tk(), [P, DEPTH * 96])
                    self.dump("der", self.DER.t[:, :, :].rearrange("p a b -> p (a b)"), self.DER.tk(), [P, DEPTH * 64])
                    self.dump("hb", HBs[0].t[:, :, :], HBs[0].tk(), [P, KC, 512])
                    self.dump("qc", QC.t[:, :, :], QC.tk(), [P, 2, TT])
                    self.dump("kvc", KVC.t[:, :], KVC.tk(), [P, TT])
                    self.dump("rsq", RSQ.t[:, :], RSQ.tk(), [P, TT])
                    self.dump("rskv", RSKV.t[:, :], RSKV.tk(), [P, TT])
                    self.dump("rskvt", RSKVT.t[:, :], RSKVT.tk(), [P, 18])
                    self.dump("kr", KR.t[64:96, :], KR.tk(), [32, TT])
                S.barrier()
            if self.ckpt("mla_inproj"):
                return
            with contextlib.ExitStack() as es1:
                WUQ = self.sb(es1, "WUQ", [P, 2, 768], BF16, gran=1 << 20)
                WUQP = self.sb(es1, "WUQP", [P, 2, 768], BF16, gran=1 << 20)
                WUKV = self.sb(es1, "WUKV", [P, 1024], BF16, gran=1 << 20)
                KTs = [self.sb(es1, "KT%d" % k, [P, 2, TT], BF16) for k in range(1)]
                VAs = [self.sb(es1, "VA%d" % k, [P, 18, 256], BF16, gran=128) for k in range(1)]
                QT = [self.sb(es1, "QT%d" % k, [P, 512], BF16, gran=512) for k in range(2)]
                PT = [self.sb(es1, "PT%d" % k, [P, 2, 512], BF16, gran=512) for k in range(3)]
                S.dma("pool", WUQ.t[:, :, :], self.d["w_uq"][i].rearrange("(k p) n -> p k n", p=P), writes=WUQ.tk())
                S.dma("pool", WUQP.t[:, :, :], self.d["w_uqp"][i].rearrange("(k p) n -> p k n", p=P), writes=WUQP.tk())
                S.dma("pool", WUKV.t[:, :], self.d["w_ukv"][i], writes=WUKV.tk())
                self.convert_ffn_weights(l)
                for VA_ in VAs:
                    self.E("pool", "memset", VA_.t[:, :, 64:192], 1.0, w=VA_.tk())
                npt = 0
                npair = 0

                def kvprep(j, hh, KT, VA):
                    h = 2 * j + hh
                    for (c0, c1) in BLOCKS_ALL:
                        n = c1 - c0
                        bk, btk = self.bank("b")
                        self.mm(self.PS.t[0:64, bk, :n], WUKV.t[:, h * P:h * P + 64], KVC.t[:, c0:c1], True, True,
                                r=WUKV.tk() + KVC.tk(0, c0, c1), w=btk)
                        self.E("dve", "tensor_tensor", KT.t[0:64, hh, c0:c1], self.PS.t[0:64, bk, :n], RSKV.t[0:64, c0:c1],
                               ALU.mult, r=btk + RSKV.tk(0, c0, c1), w=KT.tk(hh, c0, c1))
                    self.E("pool", "tensor_copy", KT.t[64:96, hh, :], KR.t[64:96, :], r=KR.tk(), w=KT.tk(hh))
                    vc0 = 0 if hh == 0 else 192
                    for t4 in range(0, 18, 4):
                        bk, btk = self.bank("b")
                        tl = list(range(t4, min(18, t4 + 4)))
                        for q_, t in enumerate(tl):
                            self.mm(self.PS.t[:, bk, q_ * 64:(q_ + 1) * 64], KVC.t[:, t * P:(t + 1) * P],
                                    WUKV.t[:, h * P + 64:(h + 1) * P], True, True,
                                    r=WUKV.tk() + KVC.tk(0, t * P, (t + 1) * P), w=btk)
                        for q_, t in enumerate(tl):
                            self.E("dve", "tensor_scalar", VA.t[:, t, vc0:vc0 + 64], self.PS.t[:, bk, q_ * 64:(q_ + 1) * 64],
                                   RSKVT.t[:, t:t + 1], self.CST.t[:, 2:3], ALU.mult, ALU.add, r=btk + RSKVT.tk(),
                                   w=VA.tk(t, hh * P, (hh + 1) * P))
                kvprep(0, 0, KTs[0], VAs[0])
                kvprep(0, 1, KTs[0], VAs[0])
                for j in range(4):
                    KT, VA = KTs[0], VAs[0]
                    items = [(hh, c0, c1) for hh in range(2) for (c0, c1) in qblocks]

                    def qprep(item, qt):
                        hh, c0, c1 = item
                        h = 2 * j + hh
                        n = c1 - c0
                        isctx = c1 <= CTXL
                        bq, btq = self.bank("b")
                        for kc in range(2):
                            self.mm(self.PS.t[0:96, bq, :n], WUQ.t[:, kc, h * 96:(h + 1) * 96], QC.t[:, kc, c0:c1],
                                    kc == 0, kc == 1, r=WUQ.tk() + QC.tk(kc, c0, c1), w=btq)
                        if isctx:
                            self.E("dve", "tensor_tensor", qt.t[0:96, :n], self.PS.t[0:96, bq, :n], RSQ.t[0:96, c0:c1],
                                   ALU.mult, r=btq + RSQ.tk(0, c0, c1), w=qt.tk())
                        else:
                            a0 = c0 - CTXL
                            self.E("dve", "tensor_tensor", qt.t[0:64, :n], self.PS.t[0:64, bq, :n], RSQ.t[0:64, c0:c1],
                                   ALU.mult, r=btq + RSQ.tk(0, c0, c1), w=qt.tk())
                            bq2, btq2 = self.bank("b")
                            for kc in range(2):
                                self.mm(self.PS.t[0:96, bq2, :n], WUQP.t[:, kc, h * 96:(h + 1) * 96], QC.t[:, kc, c0:c1],
                                        kc == 0, kc == 1, r=WUQP.tk() + QC.tk(kc, c0, c1), w=btq2)
                            t1, t1k = self.tmpf()
                            t2, t2k = self.tmpf()
                            self.E("dve", "tensor_tensor", t1[64:96, :n], self.PS.t[64:96, bq, :n], ROPE.t[64:96, 0, a0:a0 + n],
                                   ALU.mult, r=btq + ROPE.tk(), w=t1k)
                            self.E("dve", "tensor_tensor", t2[64:96, :n], self.PS.t[64:96, bq2, :n], ROPE.t[64:96, 1, a0:a0 + n],
                                   ALU.mult, r=btq2 + ROPE.tk(), w=t2k)
                            self.E("pool", "tensor_tensor", t1[64:96, :n], t1[64:96, :n], t2[64:96, :n], ALU.add,
                                   r=t1k + t2k, w=t1k)
                            self.E("pool", "tensor_tensor", qt.t[64:96, :n], t1[64:96, :n], RSQ.t[64:96, c0:c1], ALU.mult,
                                   r=t1k + RSQ.tk(0, c0, c1), w=qt.tk())

                    def attend(item, qt):
                        nonlocal npt
                        hh, c0, c1 = item
                        n = c1 - c0
                        isctx = c1 <= CTXL
                        ktiles = list(range(2)) if isctx else list(range(18))
                        bo, bto = self.bank("c")
                        pend = []
                        kpairs = [(ktiles[q], ktiles[q + 1]) for q in range(0, len(ktiles), 2)]

                        def s_issue(pr):
                            nonlocal npair
                            b0 = (npair % 2) * 2
                            npair += 1
                            btk2 = self.PS.tk(b0) + self.PS.tk(b0 + 1)
                            for k, t in enumerate(pr):
                                self.mm(self.PS.t[:, b0 + k, :n], KT.t[0:96, hh, t * P:(t + 1) * P], qt.t[0:96, :n], True, True,
                                        r=KT.tk(hh, t * P, (t + 1) * P) + qt.tk(), w=self.PS.tk(b0 + k), rg=(0, 96))
                            pend.append((pr, b0, btk2))

                        def finish():
                            nonlocal npt
                            pr, b0, btk2 = pend.pop(0)
                            pt = PT[npt % 3]
                            npt += 1
                            self.A(pt.t[:, :, :n], self.PS.t[:, b0:b0 + 2, :n], AF.Exp, scale=MLA_SCALE, r=btk2, w=pt.tk())
                            for k, t in enumerate(pr):
                                self.mm(self.PS.t[:, bo, :n], VA.t[:, t, hh * P:(hh + 1) * P], pt.t[:, k, :n],
                                        t == ktiles[0], t == ktiles[-1], r=VA.tk(t, hh * P, (hh + 1) * P) + pt.tk(), w=bto)
                        s_issue(kpairs[0])
                        for q in range(len(kpairs)):
                            if q + 1 < len(kpairs):
                                s_issue(kpairs[q + 1])
                            finish()
                        dn, dtk = self.tmpf()
                        if hh == 0:
                            self.E("dve", "tensor_copy", dn[0:64, :n], self.PS.t[64:128, bo, :n], r=bto, w=dtk)
                            self.E("dve", "reciprocal", dn[0:64, :n], dn[0:64, :n], r=dtk, w=dtk)
                            self.E("dve", "tensor_tensor", MO.t[0:64, j, c0:c1], self.PS.t[0:64, bo, :n], dn[0:64, :n],
                                   ALU.mult, r=bto + dtk, w=MO.tk(j, c0, c1))
                        else:
                            self.E("dve", "tensor_copy", dn[64:128, :n], self.PS.t[0:64, bo, :n], r=bto, w=dtk)
                            self.E("dve", "reciprocal", dn[64:128, :n], dn[64:128, :n], r=dtk, w=dtk)
                            self.E("dve", "tensor_tensor", MO.t[64:128, j, c0:c1], self.PS.t[64:128, bo, :n], dn[64:128, :n],
                                   ALU.mult, r=bto + dtk, w=MO.tk(j, c0, c1))
                    qprep(items[0], QT[0])
                    for k, item in enumerate(items):
                        if k + 1 < len(items):
                            qprep(items[k + 1], QT[(k + 1) % 2])
                        attend(item, QT[k % 2])
                        last_of_head = (k + 1 == len(items)) or (items[k + 1][0] != item[0])
                        if last_of_head and j + 1 < 4:
                            kvprep(j + 1, item[0], KT, VA)
                S.barrier()

    def gla_pair(self, l, i, jp, need_ctx, MO, RS):
        S = self.S
        NT = 18
        with contextlib.ExitStack() as es:
            GQ = self.sb(es, "GQ", [P, TT], BF16)
            GK = self.sb(es, "GK", [P, TT], BF16)
            GKT = self.sb(es, "GKT", [P, NT, P], BF16, gran=1 << 20)
            GVT = self.sb(es, "GVT", [P, NT, 256], BF16, gran=1 << 20)
            GLR = self.sb(es, "GLR", [P, TT], BF16)
            SG = self.sb(es, "SG", [P, 2, TT], BF16)
            self.E("pool", "memset", GLR.t[32:64, :], 1.0, w=GLR.tk())
            with contextlib.ExitStack() as es1:
                HBs = [self.sb(es1, "HBg%d" % k, [P, KC, 512], BF16, gran=512) for k in range(2)]
                WGq = self.sb(es1, "WGq", [P, KC, 800], BF16, gran=1 << 20)
                srcs = [(0, 416 + jp * P, P), (128, 672 + jp * P, P), (256, 928 + jp * 256, 256), (512, 1440, 32),
                        (544, 1472 + jp * 256, 256)]
                S.dma("pool", WGq.t[:, :, 0:512], self.d["ev_gla"][i, jp][:, :, 0:512], writes=WGq.tk())
                S.dma("pool", WGq.t[:, :, 512:800], self.d["ev_gla"][i, jp][:, :, 512:800], writes=WGq.tk())
                self.hblock(l, BLOCKS_ALL[0][0], BLOCKS_ALL[0][1], 0, 0, None, 0, HBs[0], 0)
                for bi_, (c0, c1) in enumerate(BLOCKS_ALL):
                    HB = HBs[bi_ % 2]
                    n = c1 - c0
                    if bi_ + 1 < len(BLOCKS_ALL):
                        self.hblock(l, BLOCKS_ALL[bi_ + 1][0], BLOCKS_ALL[bi_ + 1][1], 0, 0, None, 0, HBs[(bi_ + 1) % 2], 0)

                    def proj(col0, ncol):
                        bk, btk = self.bank("a")
                        for kc in range(KC):
                            self.mm(self.PS.t[0:ncol, bk, :n], WGq.t[:, kc, col0:col0 + ncol], HB.t[:, kc, :n],
                                    kc == 0, kc == KC - 1, r=WGq.tk() + HB.tk(kc), w=btk)
                        return bk, btk
                    bk, btk = proj(0, P)
                    self.A(GQ.t[:, c0:c1], self.PS.t[:, bk, :n], AF.Identity, scale=0.125, r=btk, w=GQ.tk(0, c0, c1))
                    bk, btk = proj(128, P)
                    self.A(GK.t[:, c0:c1], self.PS.t[:, bk, :n], AF.Identity, r=btk, w=GK.tk(0, c0, c1))
                    bk, btk = proj(512, 32)
                    self.E("dve", "tensor_copy", GLR.t[0:32, c0:c1], self.PS.t[0:32, bk, :n], r=btk, w=GLR.tk(0, c0, c1))
                    for cc in range(2):
                        bk, btk = proj(544 + cc * P, P)
                        self.A(SG.t[:, cc, c0:c1], self.PS.t[:, bk, :n], AF.Silu, r=btk, w=SG.tk(cc, c0, c1))
                    for t in range(n // P):
                        tt = c0 // P + t
                        bk, btk = self.bank("a")
                        for kc in range(KC):
                            self.mm(self.PS.t[:, bk, 0:384], HB.t[:, kc, t * P:(t + 1) * P], WGq.t[:, kc, 128:512],
                                    kc == 0, kc == KC - 1, r=WGq.tk() + HB.tk(kc), w=btk)
                        self.A(GKT.t[:, tt, :], self.PS.t[:, bk, 0:128], AF.Identity, r=btk, w=GKT.tk(tt))
                        self.E("dve", "tensor_copy", GVT.t[:, tt, :], self.PS.t[:, bk, 128:384], r=btk, w=GVT.tk(tt))
                S.barrier()
            if self.ckpt("gla_inproj"):
                return
            with contextlib.ExitStack() as es1:
                WGT = self.sb(es1, "WGT", [P, 512], BF16, gran=1 << 20)
                MSK = self.sb(es1, "MSK", [P, 4, P], F32, gran=1 << 20)
                MSK.nk = 1
                MK2 = self.sb(es1, "MK2", [P, 2, P], BF16, gran=1 << 20)
                MK2.nk = 1
                QTL = self.sb(es1, "QTL", [P, TT], BF16)
                KTL = self.sb(es1, "KTL", [P, TT], BF16)
                KH = self.sb(es1, "KH", [P, NT, P], BF16, gran=1 << 20)
                EBE = self.sb(es1, "EBE", [P, 36], F32, gran=1 << 20)
                OA = self.sb(es1, "OA", [P, 2, TT], BF16)
                SF = self.sb(es1, "SF", [P, P], F32, gran=1 << 20)
                SBB = [self.sb(es1, "SBb%d" % k, [P, P], BF16, gran=1 << 20) for k in range(2)]
                AM = [self.sb(es1, "AM%d" % k, [P, 2, P], BF16, gran=1 << 20) for k in range(2)]
                LG = [self.sb(es1, "LG%d" % k, [P, 2 * P], F32, gran=1 << 20) for k in range(2)]
                S.dma("pool", WGT.t[0:64, :], self.d["w_gate"][i], writes=WGT.tk())
                S.dma("sp", MSK.t[:, :, :], self.d["masks"][:, 0:4, :], writes=MSK.tk())
                S.dma("pool", MK2.t[:, :, :], self.d["masks"][:, 4:6, :], writes=MK2.tk())
                nlg = nam = 0
                import os
                glst = int(os.environ.get("KB_GL", "99"))
                for d in range(2 if glst >= 2 else 0):
                    for t in range(0, NT, 2):
                        c2 = slice(t * P, (t + 2) * P)
                        ctk = (0, t * P, (t + 2) * P)
                        bz, btz = self.bank("a")
                        gcol = d * 256 + jp * P
                        for u in range(2):
                            cs = slice((t + u) * P, (t + u + 1) * P)
                            self.mm(self.PS.t[:, bz, u * P:(u + 1) * P], GLR.t[0:33, cs], WGT.t[0:33, gcol:gcol + P], True, True,
                                    r=GLR.tk(*ctk) + WGT.tk(), w=btz, rg=(0, 33))
                        lg = LG[nlg % 2]
                        nlg += 1
                        self.A(lg.t[:, :], self.PS.t[:, bz, 0:2 * P], AF.Exp, scale=-1.0, r=btz, w=lg.tk())
                        self.A(lg.t[:, :], lg.t[:, :], AF.Ln, bias=1.0, scale=1.0, r=lg.tk(), w=lg.tk())
                        bb, btb = self.bank("a")
                        for u in range(2):
                            lgu = lg.t[:, u * P:(u + 1) * P]
                            self.mm(self.PS.t[:, bb, u * P:(u + 1) * P], lgu, MSK.t[:, d, :], True, True,
                                    r=lg.tk() + MSK.tk(), w=btb)
                            self.mm(self.PS.t[:, bb, (2 + u) * P:(3 + u) * P], MSK.t[:, 2 + d, :], lgu, True, True,
                                    r=lg.tk() + MSK.tk(), w=btb)
                        eb, ebk = self.tmpf()
                        eb2, ebk2 = self.tmpf()
                        self.A(eb[:, 0:2 * P], self.PS.t[:, bb, 0:2 * P], AF.Exp, r=btb, w=ebk)
                        self.A(eb[:, 2 * P:4 * P], self.PS.t[:, bb, 0:2 * P], AF.Exp, scale=-1.0, r=btb, w=ebk)
                        self.A(eb2[:, 0:2 * P], self.PS.t[:, bb, 2 * P:4 * P], AF.Exp, r=btb, w=ebk2)
                        self.E("dve", "tensor_tensor", QTL.t[:, c2], GQ.t[:, c2], eb[:, 0:2 * P], ALU.mult,
                               r=GQ.tk(*ctk) + ebk, w=QTL.tk(*ctk))
                        self.E("dve", "tensor_tensor", KTL.t[:, c2], GK.t[:, c2], eb[:, 2 * P:4 * P], ALU.mult,
                               r=GK.tk(*ctk) + ebk, w=KTL.tk(*ctk))
                        self.E("dve", "tensor_tensor", KH.t[:, t:t + 2, :].rearrange("p a b -> p (a b)"),
                               GKT.t[:, t:t + 2, :].rearrange("p a b -> p (a b)"), eb2[:, 0:2 * P], ALU.mult,
                               r=GKT.tk(t) + GKT.tk(t + 1) + ebk2, w=KH.tk(t) + KH.tk(t + 1))
                        e0 = 63 if d == 0 else 0
                        self.E("dve", "tensor_copy", EBE.t[:, 2 * t:2 * t + 4], eb[:, e0:e0 + 193:64], r=ebk, w=EBE.tk())
                    self.ckpt("gla_gate")
                    if glst == 2 or (glst == 3 and d == 1):
                        continue
                    self.E("pool", "memset", SF.t[:, :], 0.0, w=SF.tk())
                    self.E("pool", "memset", SBB[0].t[:, :], 0.0, w=SBB[0].tk())
                    if d == 0:
                        order = list(range(36))
                    else:
                        order = [3, 2, 1, 0] + list(range(35, 3, -1))
                    kvq = []

                    def kv_issue(c):
                        t = c // 2
                        p0 = (c % 2) * 64
                        bs, bts = self.bank("a")
                        for hh in range(2):
                            r0 = hh * 64
                            self.mm(self.PS.t[r0:r0 + 64, bs, 0:P], KH.t[p0:p0 + 64, t, r0:r0 + 64],
                                    GVT.t[p0:p0 + 64, t, hh * P:(hh + 1) * P], True, True, r=KH.tk(t) + GVT.tk(t), w=bts,
                                    rg=(p0, 64))
                        kvq.append((bs, bts))
                    am = None
                    am_tile = -1
                    LOOK = 2
                    for c in order[:LOOK]:
                        kv_issue(c)
                    for si_, c in enumerate(order):
                        if si_ + LOOK < len(order):
                            kv_issue(order[si_ + LOOK])
                        t = c // 2
                        p0 = (c % 2) * 64
                        cols = slice(c * 64, (c + 1) * 64)
                        sb_cur = SBB[si_ % 2]
                        sb_nxt = SBB[(si_ + 1) % 2]
                        if t != am_tile:
                            am = AM[nam % 2]
                            nam += 1
                            am_tile = t
                            ts_ = slice(t * P, (t + 1) * P)
                            for hh in range(2):
                                r0 = hh * 64
                                ba, bta = self.bank("b")
                                self.mm(self.PS.t[:, ba, 0:P], KTL.t[r0:r0 + 64, ts_], QTL.t[r0:r0 + 64, ts_],
                                        True, True, r=KTL.tk(0, t * P, (t + 1) * P) + QTL.tk(0, t * P, (t + 1) * P), w=bta,
                                        rg=(r0, 64))
                                self.E("dve", "tensor_tensor", am.t[:, hh, :], self.PS.t[:, ba, 0:P],
                                       MK2.t[:, d, :], ALU.mult, r=bta + MK2.tk(), w=am.tk())
                        need_o = need_ctx or c >= 4
                        if need_o:
                            for hh in range(2):
                                r0 = hh * 64
                                bo, bto = self.bank("c")
                                self.mm(self.PS.t[:, bo, 0:64], sb_cur.t[r0:r0 + 64, :], QTL.t[r0:r0 + 64, cols],
                                        True, False, r=sb_cur.tk() + QTL.tk(0, c * 64, (c + 1) * 64), w=bto, rg=(r0, 64))
                                self.mm(self.PS.t[:, bo, 0:64], GVT.t[p0:p0 + 64, t, hh * P:(hh + 1) * P],
                                        am.t[p0:p0 + 64, hh, p0:p0 + 64], False, True, r=GVT.tk(t) + am.tk(), w=bto, rg=(p0, 64))
                                if d == 0:
                                    self.A(OA.t[:, hh, cols], self.PS.t[:, bo, 0:64], AF.Identity, r=bto,
                                           w=OA.tk(hh, c * 64, (c + 1) * 64))
                                else:
                                    self.E("dve", "tensor_tensor", OA.t[:, hh, cols], self.PS.t[:, bo, 0:64],
                                           OA.t[:, hh, cols], ALU.add, r=bto + OA.tk(hh, c * 64, (c + 1) * 64),
                                           w=OA.tk(hh, c * 64, (c + 1) * 64))
                        bs, bts = kvq.pop(0)
                        self.E("dve", "tensor_scalar", SF.t[:, :], SF.t[:, :], EBE.t[:, c:c + 1], self.CST.t[:, 2:3],
                               ALU.mult, ALU.add, r=SF.tk() + EBE.tk(), w=SF.tk())
                        self.E("dve", "tensor_tensor", sb_nxt.t[:, :], self.PS.t[:, bs, 0:P], SF.t[:, :], ALU.add,
                               r=SF.tk() + bts, w=sb_nxt.tk())
                        self.E("dve", "tensor_tensor", SF.t[:, :], self.PS.t[:, bs, 0:P], SF.t[:, :], ALU.add,
                               r=SF.tk() + bts, w=SF.tk())
                oblocks = BLOCKS_ALL if need_ctx else BLOCKS_ALL[1:]
                if glst < 99:
                    oblocks = []
                for (c0, c1) in oblocks:
                    n = c1 - c0
                    for hh in range(2):
                        rs, rtk = self.tmpf()
                        self.stats(lambda kc: (OA.t[:, hh, c0:c1], OA.tk(hh, c0, c1)), n, 1, 128, rs[:, :n], rtk)
                        t1, t1k = self.tmpf()
                        self.E("dve", "tensor_tensor", t1[:, :n], OA.t[:, hh, c0:c1], rs[:, :n], ALU.mult,
                               r=OA.tk(hh, c0, c1) + rtk, w=t1k)
                        self.E("dve", "scalar_tensor_tensor", MO.t[:, 4 + 2 * jp + hh, c0:c1], t1[:, :n], self.col("on%d" % i, 0, 1),
                               SG.t[:, hh, c0:c1], ALU.mult, ALU.mult, r=t1k + self.COLS.tk() + SG.tk(hh, c0, c1),
                               w=MO.tk(4 + 2 * jp + hh, c0, c1))
                S.barrier()


def _consts():
    c = {}
    c["ident"] = np.eye(P, dtype=np.float32)
    n = SEQ
    row = np.repeat(np.arange(n // GRID_W), GRID_W).astype(np.float64)
    colp = np.tile(np.arange(GRID_W), n // GRID_W).astype(np.float64)
    inv = ROPE_BASE ** (-np.arange(0, 16, 2, dtype=np.float64) / 16)
    ar = row[:, None] * inv.astype(np.float32).astype(np.float64)
    ac = colp[:, None] * inv.astype(np.float32).astype(np.float64)
    ang = np.concatenate([ar, ar, ac, ac], axis=-1)
    sign = np.concatenate([-np.ones(8), np.ones(8), -np.ones(8), np.ones(8)])
    rope = np.zeros((32, 2, n), np.float32)
    rope[:, 0, :] = np.cos(ang).T
    rope[:, 1, :] = (np.sin(ang) * sign[None, :]).T
    c["ropeT"] = rope.astype(ml_dtypes.bfloat16)
    j = np.arange(P)[:, None]
    ii = np.arange(P)[None, :]
    same = (j // 64) == (ii // 64)
    m = np.zeros((P, 6, P), np.float32)
    m[:, 0, :] = np.where(same & (j <= ii), -1.0 / 16, 0.0)
    m[:, 1, :] = np.where(same & (j >= ii), -1.0 / 16, 0.0)
    m[:, 2, :] = np.where(same & (j > ii), -1.0 / 16, 0.0)
    m[:, 3, :] = np.where(same & (j < ii), -1.0 / 16, 0.0)
    m[:, 4, :] = np.where(same & (j <= ii), 1.0, 0.0)
    m[:, 5, :] = np.where(same & (j >= ii), 1.0, 0.0)
    c["masks"] = m

    def dft(L):
        k = np.arange(L)
        ph = (np.outer(k, k) % L).astype(np.float64) * (2 * np.pi / L)
        s = 1.0 / np.sqrt(L * 64.0)
        return np.stack([np.cos(ph) * s, -np.sin(ph) * s]).astype(np.float32).astype(ml_dtypes.bfloat16)
    c["dftN"] = dft(SEQ)
    c["dftC"] = dft(CTXL)
    k = np.arange(64)
    ph = (np.outer(k, k) % 64).astype(np.float64) * (2 * np.pi / 64)
    d64 = np.zeros((256, 512), np.float32)
    for g in range(4):
        d64[g * 64:(g + 1) * 64, g * 64:(g + 1) * 64] = np.cos(ph)
        d64[g * 64:(g + 1) * 64, 256 + g * 64:256 + (g + 1) * 64] = np.sin(ph)
    c["dft64"] = d64.astype(ml_dtypes.bfloat16)
    return c


def _colv(v):
    v = np.asarray(v, np.float32)
    return np.ascontiguousarray(v.reshape(-1, P).T)


def _pack_cols(inp):
    cols = np.zeros((P, NCOL), np.float32)

    def put(name, arr):
        o = COFF[name]
        cols[:, o:o + arr.shape[1]] = arr
    for l in range(DEPTH):
        put("modb%d" % l, np.repeat(_colv(inp["mod_b"][l]), 2, axis=1))
        put("pmg%d" % l, _colv(inp["pre_mix_g"][l]))
        put("qmg%d" % l, _colv(inp["post_mix_g"][l]))
        put("pfg%d" % l, _colv(inp["pre_ffn_g"][l]))
        put("qfg%d" % l, _colv(inp["post_ffn_g"][l]))
        put("fdw%d" % l, np.ascontiguousarray(np.asarray(inp["ffn_dw_w"][l]).reshape(3, NFC, P).transpose(2, 1, 0)).reshape(P, 66))
        put("fdb%d" % l, _colv(inp["ffn_dw_b"][l]))
    for i in range(2):
        put("qn%d" % i, _colv(inp["mla_q_norm"][i]))
        put("kvn%d" % i, _colv(inp["mla_kv_norm"][i]))
        put("on%d" % i, _colv(inp["gla_o_norm"][i]))
        put("cdw%d" % i, np.ascontiguousarray(np.asarray(inp["conf_dw_w"][i]).reshape(31, 6, P).transpose(2, 1, 0)).reshape(P, 186))
        put("cdb%d" % i, _colv(inp["conf_dw_b"][i]))
        put("lng%d" % i, _colv(inp["conf_ln_g"][i]))
        put("lnb%d" % i, _colv(inp["conf_ln_b"][i]))
    return cols


_PERM32 = np.concatenate([np.arange(8, 16), np.arange(0, 8), np.arange(24, 32), np.arange(16, 24)])


def _shared_inputs(inp):
    f = lambda a: np.ascontiguousarray(np.asarray(a, dtype=np.float32))
    sh = dict(_consts())
    sh["cols"] = _pack_cols(inp)
    for k in ["mod_w", "ev_in_w"]:
        sh[k] = f(inp[k])
    wi_ = f(inp["od_in_w"]).reshape(2, KC, P, 14, P)
    sh["od_in_w"] = np.ascontiguousarray(wi_.transpose(0, 3, 2, 1, 4))
    for k in ["ev_out_w", "od_out_w"]:
        w_ = f(inp[k]).reshape(2, KC, P, KC, P)
        sh[k] = np.ascontiguousarray(w_.transpose(0, 3, 2, 1, 4))
    wi = f(inp["ffn_in_w"]).reshape(DEPTH, KC, P, 2, NFC, P)
    sh["ffn_wgv"] = np.ascontiguousarray(wi.transpose(0, 4, 2, 1, 3, 5)).reshape(DEPTH, NFC, P, KC, 256)
    wo = f(inp["ffn_out_w"]).reshape(DEPTH, NFC, P, KC, P)
    sh["ffn_wo"] = np.ascontiguousarray(wo.transpose(0, 3, 2, 1, 4))
    sh["w_uq"] = f(inp["mla_w_uq"])
    sh["w_ukv"] = f(inp["mla_w_ukv"])
    ev = f(inp["ev_in_w"])
    krp = np.zeros((2, D, 96), np.float32)
    krp[:, :, 64:96] = ev[:, :, 384 + _PERM32]
    sh["w_krp"] = krp
    uq = f(inp["mla_w_uq"]).reshape(2, 256, 8, 96)
    uqp = np.zeros_like(uq)
    uqp[:, :, :, 64:96] = uq[:, :, :, 64 + _PERM32]
    sh["w_uqp"] = np.ascontiguousarray(uqp.reshape(2, 256, 768))
    gla = np.zeros((2, 2, P, KC, 800), np.float32)
    ev_r = ev.reshape(2, KC, P, 1984)
    for jp in range(2):
        for (o, c, w) in [(0, 416 + jp * P, P), (128, 672 + jp * P, P), (256, 928 + jp * 256, 256), (512, 1440, 32),
                          (544, 1472 + jp * 256, 256)]:
            gla[:, jp, :, :, o:o + w] = ev_r[:, :, :, c:c + w].transpose(0, 2, 1, 3)
    sh["ev_gla"] = gla
    wg = np.zeros((2, 64, 512), np.float32)
    wg[:, 0:16, 0:256] = f(inp["gla_w_gate_fw"])
    wg[:, 16:32, 256:512] = f(inp["gla_w_gate_bw"])
    wg[:, 32, 0:256] = f(inp["gla_b_gate_fw"])
    wg[:, 32, 256:512] = f(inp["gla_b_gate_bw"])
    sh["w_gate"] = wg
    return sh


def _core_inputs(inp, b):
    x = np.asarray(inp["x"], np.float32)
    ctx = np.asarray(inp["ctx"], np.float32)
    cc = np.zeros((P, 8, 2), np.float32)
    cc[:, :, 0] = _colv(inp["c"][b])
    cc[:, :, 1] = _colv(inp["c_ctx"])
    return {"xT": np.ascontiguousarray(x[b].T), "ctxT": np.ascontiguousarray(ctx[b].T),
            "ccol": np.ascontiguousarray(cc.reshape(P, 16))}


_NC_CACHE = {}


def kernel(**inputs):
    inp = {k: np.asarray(v) for k, v in inputs.items()}
    nb = inp["x"].shape[0]
    if "nc" not in _NC_CACHE:
        _NC_CACHE["nc"] = KB().build()
    nc = _NC_CACHE["nc"]
    sh = _shared_inputs(inp)
    in_maps = []
    for b in range(nb):
        m = dict(sh)
        m.update(_core_inputs(inp, b))
        in_maps.append(m)
    res = run_bass_kernel_spmd(nc, in_maps, core_ids=list(range(nb)))
    out = np.stack([np.ascontiguousarray(np.asarray(r["outT"], dtype=np.float32).T) for r in res.results], axis=0)
    return out
```

```python
import contextlib
import numpy as np
import ml_dtypes
import concourse.bass as bass
import concourse.mybir as mybir
from concourse.bass_utils import run_bass_kernel_spmd

F32 = mybir.dt.float32
BF16 = mybir.dt.bfloat16
AF = mybir.ActivationFunctionType
ALU = mybir.AluOpType

P = 128
D = 1024
KC = 8
SEQ = 2048
CTXL = 256
TT = SEQ + CTXL
DEPTH = 4
DFF = 2816
NFC = 22
EPS = 1e-6
MLA_SCALE = 96.0 ** -0.5
GRID_W = 64
ROPE_BASE = 10000.0


class Tok:
    __slots__ = ("w", "r", "excl")

    def __init__(self, excl=False):
        self.w = None
        self.r = {}
        self.excl = excl


class Sched:
    ENGS = ("pe", "act", "dve", "pool", "sp")

    def __init__(self, nc, es, n_dma_sems=40, same_engine_sync=True):
        self.nc = nc
        self.same = same_engine_sync
        self.ops = {e: [] for e in self.ENGS}
        self.cnt = {e: 0 for e in self.ENGS}
        self.sems = {}
        for e in self.ENGS:
            self.sems[e] = es.enter_context(nc.semaphore("s_" + e))
        self.dsems = []
        for i in range(n_dma_sems):
            k = "d%d" % i
            self.sems[k] = es.enter_context(nc.semaphore("s_" + k))
            self.dsems.append(k)
        self.dtot = {k: 0 for k in self.dsems}
        nsw = (2 * n_dma_sems) // 3
        self.dpool = {"sw": self.dsems[:nsw], "hw": self.dsems[nsw:]}
        self.drr = {"sw": 0, "hw": 0}
        self.known = {e: {} for e in self.ENGS}
        self.out_waits = []
        self.ninstr = 0

    def _wait(self, eng, k, v):
        kn = self.known[eng]
        if kn.get(k, 0) < v:
            kn[k] = v
            self.ops[eng].append(("wait", k, v))

    def _need(self, eng, reads, writes):
        need = {}

        def add(dep):
            if dep is None:
                return
            k, v = dep
            if k == eng and (eng == "pe" or not self.same):
                return
            if need.get(k, 0) < v:
                need[k] = v
        for t in reads:
            add(t.w)
        for t in writes:
            add(t.w)
            for k, v in t.r.items():
                add((k, v))
        for k, v in need.items():
            self._wait(eng, k, v)

    def op(self, eng, fn, reads=(), writes=()):
        ex = [t for t in reads if t.excl]
        if ex:
            writes = list(writes) + ex
            reads = [t for t in reads if not t.excl]
        self._need(eng, reads, writes)
        self.cnt[eng] += 1
        n = self.cnt[eng]
        self.ops[eng].append(("ins", fn, eng, 1))
        for t in reads:
            if t.r.get(eng, 0) < n:
                t.r[eng] = n
        for t in writes:
            t.w = (eng, n)
            t.r = {}
        self.ninstr += 1

    def dma(self, q, out, in_, reads=(), writes=(), is_output=False):
        pk = "sw" if q == "pool" else "hw"
        lst = self.dpool[pk]
        k = lst[self.drr[pk]]
        self.drr[pk] = (self.drr[pk] + 1) % len(lst)
        self._wait(q, k, self.dtot[k])
        self._need(q, reads, writes)
        self.dtot[k] += 16
        v = self.dtot[k]
        self.ops[q].append(("ins", lambda e, o=out, i=in_: e.dma_start(out=o, in_=i), k, 16))
        for t in reads:
            if t.r.get(k, 0) < v:
                t.r[k] = v
        for t in writes:
            t.w = (k, v)
            t.r = {}
        if is_output:
            self.out_waits.append((k, v))
        self.ninstr += 1

    def barrier(self):
        for e in self.ENGS:
            for f in self.ENGS:
                if f != e and self.cnt[f] > 0:
                    self._wait(e, f, self.cnt[f])
            for k in self.dsems:
                if self.dtot[k] > 0:
                    self._wait(e, k, self.dtot[k])

    def emit(self):
        nc = self.nc
        for k, v in self.out_waits:
            self._wait("sp", k, v)
        sems = self.sems

        def replay(e, name):
            for o in self.ops[name]:
                if o[0] == "wait":
                    e.wait_ge(sems[o[1]], o[2])
                else:
                    o[1](e).then_inc(sems[o[2]], o[3])
        with nc.Block() as block:
            @block.sync
            def _(e):
                replay(e, "sp")

            @block.tensor
            def _(e):
                replay(e, "pe")

            @block.scalar
            def _(e):
                replay(e, "act")

            @block.vector
            def _(e):
                replay(e, "dve")

            @block.gpsimd
            def _(e):
                replay(e, "pool")


class Buf:
    def __init__(self, t, shape, gran=256, excl=False):
        self.t = t
        self.excl = excl
        self.shape = shape
        self.gran = gran
        self.ncols = shape[-1]
        self.nk = shape[1] if len(shape) == 3 else 1
        self.toks = {}

    def _tok(self, kc, g):
        key = (kc, g)
        t = self.toks.get(key)
        if t is None:
            t = Tok(self.excl)
            self.toks[key] = t
        return t

    def tk(self, kc=None, c0=0, c1=None):
        if c1 is None:
            c1 = self.ncols
        if c1 <= c0:
            return []
        kcs = range(self.nk) if kc is None else ([kc] if isinstance(kc, int) else list(kc))
        gs = range(c0 // self.gran, (c1 - 1) // self.gran + 1)
        return [self._tok(k, g) for k in kcs for g in gs]


def cols_layout():
    off = {}
    n = 0

    def add(name, w):
        nonlocal n
        off[name] = n
        n += w
    for l in range(DEPTH):
        add("modb%d" % l, 96)
        add("pmg%d" % l, 8)
        add("qmg%d" % l, 8)
        add("pfg%d" % l, 8)
        add("qfg%d" % l, 8)
        add("fdw%d" % l, 66)
        add("fdb%d" % l, 22)
    for i in range(2):
        add("qn%d" % i, 2)
        add("kvn%d" % i, 1)
        add("on%d" % i, 1)
        add("cdw%d" % i, 186)
        add("cdb%d" % i, 6)
        add("lng%d" % i, 6)
        add("lnb%d" % i, 6)
    return off, n


COFF, NCOL = cols_layout()
BLOCKS_ALL = [(0, 256), (256, 768), (768, 1280), (1280, 1792), (1792, 2304)]


class KB:
    def __init__(self, n_layers=DEPTH, dumps=()):
        self.n_layers = n_layers
        self.nc = nc = bass.Bass("TRN2", target_bir_lowering=False)
        self.dumps = dumps
        BFN = ("dftN", "dftC", "dft64", "ropeT")
        dr = lambda name, shape, kind="ExternalInput": nc.dram_tensor(name, shape, BF16 if name in BFN else F32, kind=kind).ap()
        self.d = {}
        for name, shape in [
            ("xT", [D, SEQ]), ("ctxT", [D, CTXL]), ("ccol", [P, 16]), ("cols", [P, NCOL]),
            ("mod_w", [DEPTH, D, 6 * D]), ("ev_in_w", [2, D, 1984]), ("w_krp", [2, D, 96]),
            ("w_uq", [2, 256, 768]), ("w_uqp", [2, 256, 768]), ("w_ukv", [2, 128, 1024]),
            ("w_gate", [2, 64, 512]), ("ev_out_w", [2, KC, P, KC, P]), ("od_in_w", [2, 14, P, KC, P]),
            ("od_out_w", [2, KC, P, KC, P]), ("ffn_wgv", [DEPTH, NFC, P, KC, 256]), ("ffn_wo", [DEPTH, KC, P, NFC, P]),
            ("ev_gla", [2, 2, P, KC, 800]), ("ident", [P, P]), ("ropeT", [32, 2, SEQ]), ("masks", [P, 6, P]),
            ("dftN", [2, SEQ, SEQ]), ("dftC", [2, CTXL, CTXL]), ("dft64", [256, 512]),
        ]:
            self.d[name] = dr(name, shape)
        self.outT = dr("outT", [D, SEQ], kind="ExternalOutput")
        self.wgv_bf = nc.dram_tensor("wgv_bf", [DEPTH, NFC, P, KC, 256], BF16, kind="Internal").ap()
        self.wgv_tok = {}
        self.dump_aps = {}

    def sb(self, es, name, shape, dt, gran=256):
        self._nm = getattr(self, "_nm", 0) + 1
        t = es.enter_context(self.nc.sbuf_tensor("%s_%d" % (name, self._nm), shape, dt))
        return Buf(t, shape, gran)

    def E(self, eng, method, *args, r=(), w=(), **kw):
        self.S.op(eng, lambda e: getattr(e, method)(*args, **kw), reads=r, writes=w)

    def A(self, out, in_, func, r=(), w=(), **kw):
        self.S.op("act", lambda e: e.activation(out, in_, func, **kw), reads=r, writes=w)

    def mm(self, out, lhsT, rhs, start, stop, r=(), w=(), rg=(0, 128)):
        groups = frozenset(range(rg[0] // 32, (rg[0] + rg[1] + 31) // 32))
        S = self.S
        for t in w:
            prev = self._bank_rg.get(id(t))
            if prev is not None and prev[0] != groups:
                S._wait("pe", "pe", prev[1])
        S.op("pe", lambda e: e.matmul(out, lhsT, rhs, start=start, stop=stop), reads=r, writes=w)
        for t in w:
            self._bank_rg[id(t)] = (groups, S.cnt["pe"])

    def bank(self, pool):
        lst = {"a": (0, 1, 2, 3), "b": (4, 5), "c": (6, 7), "g": (0, 1, 5), "v": (2, 3, 4), "x": (5, 6)}[pool]
        i = self._bk.get(pool, 0)
        self._bk[pool] = i + 1
        b = lst[i % len(lst)]
        return b, self.PS.tk(b)

    def tmpf(self):
        i = self._tf
        self._tf += 1
        b = self.TF[i % len(self.TF)]
        return b.t, b.tk()

    def tmpr(self):
        i = self._tr
        self._tr += 1
        b = self.TR[i % len(self.TR)]
        return b.t, b.tk()

    def tmpb(self):
        i = self._tb
        self._tb += 1
        b = self.TB[i % len(self.TB)]
        return b.t, b.tk()

    def col(self, name, j0=0, w=1):
        o = COFF[name] + j0
        return self.COLS.t[:, o:o + w]

    def xsrc(self, kc, c0, c1):
        if c1 <= CTXL:
            return self.CT.t[:, kc, c0:c1], self.CT.tk(kc, c0, c1)
        assert c0 >= CTXL
        return self.X.t[:, kc, c0 - CTXL:c1 - CTXL], self.X.tk(kc, c0 - CTXL, c1 - CTXL)

    def modv(self, l, w, kind):
        return self.DER.t[:, (l * 2 + w) * 4 + kind, :]

    def modb(self, l, w, which):
        return self.MOD.t[:, l, which * 8:(which + 1) * 8, w]

    def convert_ffn_weights(self, l):
        for fc in range(NFC):
            t = Tok()
            self.wgv_tok[(l, fc)] = t
            self.S.dma("pool", self.wgv_bf[l, fc], self.d["ffn_wgv"][l, fc], writes=[t])

    def ckpt(self, name):
        import os
        if not hasattr(self, "marks"):
            self.marks = []
        self.marks.append((name, self.S.cnt["pe"]))
        if os.environ.get("KB_STOP") == name:
            print("stopped at", name)
            self.stop = True
        return self.stop

    def dump(self, name, ap, toks, shape):
        o = self.nc.dram_tensor("dbg_" + name, shape, F32, kind="ExternalOutput").ap()
        self.S.dma("pool", o, ap, reads=toks, is_output=True)

    def build(self):
        nc = self.nc
        with contextlib.ExitStack() as es:
            self.S = Sched(nc, es)
            self._bk = {}
            self._bank_rg = {}
            self._tf = 0
            self._tr = 0
            self._tb = 0
            pst = es.enter_context(nc.psum_tensor("ps", [P, 8, 512], F32))
            self.PS = Buf(pst, [P, 8, 512], gran=512, excl=True)
            self.X = self.sb(es, "X", [P, KC, SEQ], F32)
            self.CT = self.sb(es, "CT", [P, KC, CTXL], F32)
            self.COLS = self.sb(es, "COLS", [P, NCOL], F32, gran=1 << 20)
            self.MOD = self.sb(es, "MOD", [P, DEPTH, 48, 2], F32, gran=1 << 20)
            self.MOD.nk = 1
            self.DER = self.sb(es, "DER", [P, DEPTH * 2 * 4, 8], F32, gran=1 << 20)
            self.DER.nk = 1
            self.IDENT = self.sb(es, "IDENT", [P, P], BF16)
            self.ONES = self.sb(es, "ONES", [P, P], BF16)
            self.TF = [self.sb(es, "TF%d" % i, [P, 512], F32, gran=512) for i in range(4)]
            self.TB = [self.sb(es, "TB%d" % i, [P, 512], BF16, gran=512) for i in range(4)]
            self.TR = [self.sb(es, "TR%d" % i, [P, 512], F32, gran=512) for i in range(3)]
            S = self.S
            for kc in range(KC):
                S.dma("sp", self.X.t[:, kc, :], self.d["xT"][kc * P:(kc + 1) * P, :], writes=self.X.tk(kc))
            S.dma("sp", self.CT.t[:, :, :], self.d["ctxT"].rearrange("(k p) n -> p k n", p=P), writes=self.CT.tk())
            S.dma("sp", self.COLS.t[:, :], self.d["cols"], writes=self.COLS.tk())
            S.dma("pool", self.IDENT.t[:, :], self.d["ident"], writes=self.IDENT.tk())
            self.E("pool", "memset", self.ONES.t[:, :], 1.0, w=self.ONES.tk())
            self.CST = self.sb(es, "CST", [P, 4], F32, gran=1 << 20)
            self.E("pool", "memset", self.CST.t[:, 0:1], 1.0, w=self.CST.tk())
            self.E("pool", "memset", self.CST.t[:, 1:2], 1.0 / 768, w=self.CST.tk())
            self.E("pool", "memset", self.CST.t[:, 2:3], 0.0, w=self.CST.tk())
            import os
            if os.environ.get('KB_SKIP_MOD') != '1':
                self.modulation(es)
            self.stop = False
            for l in range(self.n_layers):
                need_ctx = l < 2
                if self.stop:
                    break
                if l % 2 == 0:
                    self.even_mixer(l, need_ctx)
                else:
                    self.odd_mixer(l, need_ctx)
                if self.stop or self.ckpt("mixer%d" % l):
                    break
                self.ffn(l, need_ctx)
                if self.ckpt("ffn%d" % l):
                    break
            S.barrier()
            for kc in range(KC):
                S.dma("sp", self.outT[kc * P:(kc + 1) * P, :], self.X.t[:, kc, :], reads=self.X.tk(kc), is_output=True)
            if "ctx" in self.dumps:
                self.dump("ctx", self.CT.t[:, :, :], self.CT.tk(),
                          [P, KC, CTXL])
            S.emit()
        return nc

    def modulation(self, es0):
        S = self.S
        SC = self.sb(es0, "SC", [P, KC, 2], BF16, gran=1 << 20)
        SC.nk = 1
        self.SC = SC
        with contextlib.ExitStack() as es:
            CC = self.sb(es, "CC", [P, 16], F32, gran=1 << 20)
            WM = [self.sb(es, "WM%d" % i, [P, KC, 512], BF16, gran=512) for i in range(3)]
            S.dma("sp", CC.t[:, :], self.d["ccol"], writes=CC.tk())
            self.A(SC.t[:, :, :], CC.t[:, :], AF.Silu, r=CC.tk(), w=SC.tk())
            nw = 0
            import os
            stage = int(os.environ.get('KB_MOD_STAGE', '9'))
            for l in range(1 if stage >= 2 else 0):
                bank, btk = self.bank("b")
                for q in range(12):
                    wm = WM[nw % 3]
                    nw += 1
                    S.dma("pool", wm.t[:, :, :],
                          self.d["mod_w"][l, :, q * 512:(q + 1) * 512].rearrange("(k p) n -> p k n", p=P),
                          writes=wm.tk())
                    for jj in range(4):
                        j = q * 4 + jj
                        for kc in range(KC):
                            self.mm(self.PS.t[:, bank, 2 * j:2 * j + 2], wm.t[:, kc, jj * P:(jj + 1) * P],
                                    SC.t[:, kc, :], kc == 0, kc == KC - 1, r=wm.tk() + SC.tk(), w=btk)
                if stage < 3:
                    continue
                self.mod_final(l, bank, btk)
            S.barrier()

    def mod_final(self, l, bank, btk):
        self.E("dve", "tensor_tensor", self.MOD.t[:, l, :, :].rearrange("p a b -> p (a b)"), self.PS.t[:, bank, 0:96],
               self.col("modb%d" % l, 0, 96), ALU.add, r=btk + self.COLS.tk(), w=self.MOD.tk())
        for w in range(2):
            for kind, (mi, gname, plus1) in enumerate([(1, "pmg", True), (2, "qmg", False),
                                                       (4, "pfg", True), (5, "qfg", False)]):
                src = self.MOD.t[:, l, mi * 8:(mi + 1) * 8, w]
                g = self.col("%s%d" % (gname, l), 0, 8)
                dst = self.modv(l, w, kind)
                if plus1:
                    self.E("dve", "scalar_tensor_tensor", dst, src, self.CST.t[:, 0:1], g, ALU.add, ALU.mult,
                           r=self.MOD.tk() + self.COLS.tk() + self.CST.tk(), w=self.DER.tk())
                else:
                    self.E("dve", "tensor_tensor", dst, src, g, ALU.mult,
                           r=self.MOD.tk() + self.COLS.tk(), w=self.DER.tk())

    def mod_tasks(self, l, wm):
        S = self.S
        SC = self.SC
        bank = 7
        btk = self.PS.tk(bank)
        sched = {}
        for q in range(12):
            def dma(q=q):
                S.dma("pool", wm.t[:, :, :],
                      self.d["mod_w"][l, :, q * 512:(q + 1) * 512].rearrange("(k p) n -> p k n", p=P), writes=wm.tk())

            def mms(q=q):
                for jj in range(4):
                    j = q * 4 + jj
                    for kc in range(KC):
                        self.mm(self.PS.t[:, bank, 2 * j:2 * j + 2], wm.t[:, kc, jj * P:(jj + 1) * P],
                                SC.t[:, kc, :], kc == 0, kc == KC - 1, r=wm.tk() + SC.tk(), w=btk)
            sched[2 + 5 * q] = dma
            sched[6 + 5 * q] = mms
        sched[6 + 5 * 11 + 2] = lambda: self.mod_final(l, bank, btk)
        return sched

    def stats(self, srcfn, n, nchunks, dim, out_ap, out_tk, rows=(0, P)):
        bank, btk = self.bank("b")
        for kc in range(nchunks):
            src, stk = srcfn(kc)
            sq, sqtk = self.tmpb()
            self.A(sq[:, :n], src, AF.Square, r=stk, w=sqtk)
            self.mm(self.PS.t[:, bank, :n], self.ONES.t[:, :], sq[:, :n], kc == 0, kc == nchunks - 1,
                    r=sqtk + self.ONES.tk(), w=btk)
        self.rstd_from_ps(bank, btk, n, dim, out_ap, out_tk, rows)

    def rstd_from_ps(self, bank, btk, n, dim, out_ap, out_tk, rows=(0, P)):
        tmp, ttk = self.tmpf()
        r0, r1 = rows
        self.A(tmp[r0:r1, :n], self.PS.t[r0:r1, bank, :n], AF.Ln, bias=EPS, scale=1.0 / dim,
               r=btk, w=ttk)
        self.A(out_ap, tmp[r0:r1, :n], AF.Exp, scale=-0.5, r=ttk, w=out_tk)

    def hblock(self, l, c0, c1, kind_a, which_b, RS, rc0, HB, o0):
        n = c1 - c0
        if RS is None:
            rs, rtk = self.tmpr()
            RS = Buf(rs, [P, 512], 512)
            RS.toks[(0, 0)] = rtk[0]
            rc0 = 0
            self.stats(lambda kc: self.xsrc(kc, c0, c1), n, KC, D, rs[:, :n], rtk)
        w = 1 if c1 <= CTXL else 0
        A_ = self.modv(l, w, kind_a)
        B_ = self.modb(l, w, which_b)
        for kc in range(KC):
            src, stk = self.xsrc(kc, c0, c1)
            tmp, ttk = self.tmpf()
            self.E("dve", "tensor_tensor", tmp[:, :n], src, RS.t[:, rc0:rc0 + n], ALU.mult,
                   r=stk + RS.tk(0, rc0, rc0 + n), w=ttk)
            self.E("dve", "tensor_scalar", HB.t[:, kc, o0:o0 + n], tmp[:, :n], A_[:, kc:kc + 1], B_[:, kc:kc + 1],
                   ALU.mult, ALU.add, r=ttk + self.DER.tk() + self.MOD.tk(), w=HB.tk(kc, o0, o0 + n))

    def post_residual(self, l, YB, y0, c0, c1, kind_g):
        n = c1 - c0
        w = 1 if c1 <= CTXL else 0
        G_ = self.modv(l, w, kind_g)
        rs, rtk = self.tmpr()
        self.stats(lambda kc: (YB.t[:, kc, y0:y0 + n], YB.tk(kc, y0, y0 + n)), n, KC, D, rs[:, :n], rtk)
        for kc in range(KC):
            xs, xtk = self.xsrc(kc, c0, c1)
            tmp, ttk = self.tmpf()
            self.E("dve", "tensor_tensor", tmp[:, :n], YB.t[:, kc, y0:y0 + n], rs[:, :n], ALU.mult,
                   r=YB.tk(kc, y0, y0 + n) + rtk, w=ttk)
            self.E("dve", "scalar_tensor_tensor", xs, tmp[:, :n], G_[:, kc:kc + 1], xs, ALU.mult, ALU.add,
                   r=ttk + xtk + self.DER.tk(), w=xtk)

    def out_proj(self, l, wname, i, MO, blocks, es_outer):
        S = self.S
        with contextlib.ExitStack() as es:
            WO = self.sb(es, "WO", [P, KC, D], BF16, gran=P)
            YB = self.sb(es, "YB", [P, KC, 512], F32, gran=512)
            for oc in range(KC):
                S.dma("pool", WO.t[:, :, oc * P:(oc + 1) * P], self.d[wname][i, oc], writes=WO.tk(None, oc * P, (oc + 1) * P))
            for (c0, c1) in blocks:
                n = c1 - c0
                for oc in range(KC):
                    bank, btk = self.bank("a")
                    for kc in range(KC):
                        self.mm(self.PS.t[:, bank, :n], WO.t[:, kc, oc * P:(oc + 1) * P], MO.t[:, kc, c0:c1],
                                kc == 0, kc == KC - 1, r=WO.tk(kc, oc * P, (oc + 1) * P) + MO.tk(kc, c0, c1), w=btk)
                    self.A(YB.t[:, oc, :n], self.PS.t[:, bank, :n], AF.Identity, r=btk, w=YB.tk(oc, 0, n))
                self.post_residual(l, YB, 0, c0, c1, 1)
            S.barrier()

    def ffn(self, l, need_ctx):
        S = self.S
        lat = [(CTXL + t0, min(410, SEQ - t0), CTXL, TT) for t0 in range(0, SEQ, 410)]
        ctxs = [(0, CTXL, 0, CTXL)]
        if need_ctx:
            groups = [ctxs + lat[0:1], lat[1:3], lat[3:5]]
        else:
            groups = [lat[0:2], lat[2:4], lat[4:5]]
        with contextlib.ExitStack() as es:
            HF = self.sb(es, "HF", [P, KC, 832], BF16)
            U = self.sb(es, "U", [P, NFC, 820], BF16, gran=1 << 20)
            YB = self.sb(es, "YBF", [P, KC, 820], F32, gran=1 << 20)
            WG = [self.sb(es, "WG%d" % i, [P, KC, 256], BF16, gran=1 << 20) for i in range(2)]
            WOt = [self.sb(es, "WOt%d" % i, [P, NFC, P], BF16, gran=1 << 20) for i in range(2)]
            DG = [self.sb(es, "DG%d" % i, [P, 3, P], BF16, gran=1 << 20) for i in range(3)]
            GSB = [self.sb(es, "GSB%d" % i, [P, 512], BF16, gran=512) for i in range(3)]
            nwg = nwo = ngs = ndg = 0
            msched = {}
            if l + 1 < self.n_layers:
                WMf = self.sb(es, "WMf", [P, KC, 512], BF16, gran=512)
                msched = self.mod_tasks(l + 1, WMf)
            mstep = 0
            HL = self.sb(es, "HL", [P, KC, 8], BF16, gran=1 << 20)
            for gi, grp in enumerate(groups):
                c0, n, lo, hi = grp[0]
                if gi > 0 and c0 - 1 >= lo:
                    self.hblock(l, c0 - 1, c0, 2, 3, None, 0, HL, gi)
            for gi, grp in enumerate(groups):
                hoff = []
                ho = 0
                for si, (c0, n, lo, hi) in enumerate(grp):
                    e0 = max(lo, c0 - 1)
                    e1 = min(hi, c0 + n + 1)
                    self.hblock(l, e0, e1, 2, 3, None, 0, HF, ho)
                    if si == 0 and gi > 0 and c0 - 1 >= lo:
                        self.E("dve", "tensor_copy", HF.t[:, :, ho:ho + 1], HL.t[:, :, gi:gi + 1],
                               r=HL.tk(), w=HF.tk(None, ho, ho + 1))
                    hoff.append((ho, e0, e1))
                    ho += e1 - e0
                self.ckpt("f_h")
                uoffs = []
                uo = 0
                for (c0, n, lo, hi) in grp:
                    uoffs.append(uo)
                    uo += n
                fcbuf = {}
                dgbuf = {}

                def wload(fc):
                    nonlocal nwg
                    if fc >= NFC or fc in fcbuf:
                        return
                    wg = WG[nwg % 2]
                    nwg += 1
                    fcbuf[fc] = wg
                    S.dma("sp", wg.t[:, :, :], self.wgv_bf[l, fc], reads=[self.wgv_tok[(l, fc)]], writes=wg.tk())

                def dbuild(fc):
                    nonlocal ndg
                    if fc >= NFC or fc in dgbuf:
                        return
                    dg = DG[ndg % 3]
                    ndg += 1
                    dgbuf[fc] = dg
                    for k in range(3):
                        self.E("dve", "tensor_scalar", dg.t[:, k, :], self.IDENT.t[:, :],
                               self.col("fdw%d" % l, fc * 3 + k, 1), self.CST.t[:, 2:3], ALU.mult, ALU.add,
                               r=self.IDENT.tk() + self.COLS.tk(), w=dg.tk())

                def prep(fc, si):
                    wload(fc)
                    wload(fc + 1)
                    wg = fcbuf[fc]
                    c0, n, lo, hi = grp[si]
                    ho, e0, e1 = hoff[si]
                    ne = e1 - e0
                    bank, btk = self.bank("g")
                    for kc in range(KC):
                        self.mm(self.PS.t[:, bank, :ne], wg.t[:, kc, 0:P], HF.t[:, kc, ho:ho + ne],
                                kc == 0, kc == KC - 1, r=wg.tk() + HF.tk(kc, ho, ho + ne), w=btk)
                    bank3, btk3 = self.bank("v")
                    vo = ho + (c0 - e0)
                    for kc in range(KC):
                        self.mm(self.PS.t[:, bank3, :n], wg.t[:, kc, P:2 * P], HF.t[:, kc, vo:vo + n],
                                kc == 0, kc == KC - 1, r=wg.tk() + HF.tk(kc, vo, vo + n), w=btk3)
                    return [bank, btk, bank3, btk3]

                def conv(fc, si, st):
                    bank, btk, bank3, btk3 = st
                    c0, n, lo, hi = grp[si]
                    ho, e0, e1 = hoff[si]
                    off = c0 - e0
                    cbuf, ctk = self.tmpf()
                    self.A(cbuf[:, :n], self.PS.t[:, bank, off:off + n], AF.Identity,
                           bias=self.col("fdb%d" % l, fc, 1), scale=self.col("fdw%d" % l, fc * 3 + 1, 1),
                           r=btk + self.COLS.tk(), w=ctk)
                    a = 1 if off == 0 else 0
                    self.E("dve", "scalar_tensor_tensor", cbuf[:, a:n], self.PS.t[:, bank, off - 1 + a:off - 1 + n],
                           self.col("fdw%d" % l, fc * 3 + 0, 1), cbuf[:, a:n], ALU.mult, ALU.add,
                           r=btk + ctk + self.COLS.tk(), w=ctk)
                    b = n - 1 if e1 == c0 + n else n
                    self.E("dve", "scalar_tensor_tensor", cbuf[:, 0:b], self.PS.t[:, bank, off + 1:off + 1 + b],
                           self.col("fdw%d" % l, fc * 3 + 2, 1), cbuf[:, 0:b], ALU.mult, ALU.add,
                           r=btk + ctk + self.COLS.tk(), w=ctk)
                    st += [cbuf, ctk]

                def fin(fc, si, st):
                    bank, btk, bank3, btk3, cbuf, ctk = st
                    c0, n, lo, hi = grp[si]
                    uo = uoffs[si]
                    ge, getk = self.tmpb()
                    self.A(ge[:, :n], cbuf[:, :n], AF.Gelu, r=ctk, w=getk)
                    self.E("dve", "tensor_tensor", U.t[:, fc, uo:uo + n], self.PS.t[:, bank3, :n], ge[:, :n],
                           ALU.mult, r=getk + btk3, w=U.tk(fc))
                items = [(fc, si) for fc in range(NFC) for si in range(len(grp))]
                NI = len(items)
                sts = {}
                for k in range(NI + 2):
                    if k < NI:
                        if mstep in msched:
                            msched.pop(mstep)()
                        mstep += 1
                        sts[k] = prep(*items[k])
                    if 0 <= k - 1 < NI:
                        conv(items[k - 1][0], items[k - 1][1], sts[k - 1])
                    if 0 <= k - 2 < NI:
                        fin(items[k - 2][0], items[k - 2][1], sts[k - 2])
                        del sts[k - 2]
                self.ckpt("f_main")
                for oc in range(KC):
                    wo = WOt[nwo % 2]
                    nwo += 1
                    S.dma("pool", wo.t[:, :, :], self.d["ffn_wo"][l, oc], writes=wo.tk())
                    uo = 0
                    for (c0, n, lo, hi) in grp:
                        bank, btk = self.bank("a")
                        for fc in range(NFC):
                            self.mm(self.PS.t[:, bank, :n], wo.t[:, fc, :], U.t[:, fc, uo:uo + n],
                                    fc == 0, fc == NFC - 1, r=wo.tk() + U.tk(fc), w=btk)
                        self.A(YB.t[:, oc, uo:uo + n], self.PS.t[:, bank, :n], AF.Identity, r=btk, w=YB.tk(oc))
                        uo += n
                self.ckpt("f_out")
                uo = 0
                for (c0, n, lo, hi) in grp:
                    self.post_residual(l, YB, uo, c0, c0 + n, 3)
                    uo += n
            for kk in sorted(msched):
                msched.pop(kk)()
            S.barrier()

    def odd_mixer(self, l, need_ctx):
        S = self.S
        i = l // 2
        blocks = BLOCKS_ALL if need_ctx else BLOCKS_ALL[1:]
        PADW = 15
        with contextlib.ExitStack() as es:
            MO = self.sb(es, "MOo", [P, KC, TT], BF16)
            UC = self.sb(es, "UC", [P, 6, SEQ + 2 * PADW], BF16)
            UCC = self.sb(es, "UCC", [P, 6, CTXL + 2 * PADW], BF16, gran=1 << 20)
            for c in range(6):
                self.E("pool", "memset", UC.t[:, c, 0:PADW], 0.0, w=UC.tk(c, 0, PADW))
                self.E("pool", "memset", UC.t[:, c, SEQ + PADW:SEQ + 2 * PADW], 0.0,
                       w=UC.tk(c, SEQ + PADW, SEQ + 2 * PADW))
                if need_ctx:
                    self.E("pool", "memset", UCC.t[:, c, 0:PADW], 0.0, w=UCC.tk(c))
                    self.E("pool", "memset", UCC.t[:, c, CTXL + PADW:CTXL + 2 * PADW], 0.0, w=UCC.tk(c))
            with contextlib.ExitStack() as es1:
                FT = self.sb(es1, "FT", [P, 18, 256], BF16, gran=1 << 20)
                with contextlib.ExitStack() as es2:
                    HB = self.sb(es2, "HB", [P, KC, 512], BF16, gran=512)
                    WI = self.sb(es2, "WI", [P, KC, 1792], BF16, gran=P)
                    for j_ in [0, 6, 1, 7, 2, 8, 3, 9, 4, 10, 5, 11, 12, 13]:
                        S.dma("pool", WI.t[:, :, j_ * P:(j_ + 1) * P], self.d["od_in_w"][i, j_],
                              writes=WI.tk(None, j_ * P, (j_ + 1) * P))
                    self.convert_ffn_weights(l)
                    for (c0, c1) in blocks:
                        n = c1 - c0
                        isctx = c1 <= CTXL
                        self.hblock(l, c0, c1, 0, 0, None, 0, HB, 0)
                        for c in range(6):
                            ba, bta = self.bank("a")
                            for kc in range(KC):
                                self.mm(self.PS.t[:, ba, :n], WI.t[:, kc, c * P:(c + 1) * P], HB.t[:, kc, :n],
                                        kc == 0, kc == KC - 1, r=WI.tk(kc, c * P, (c + 1) * P) + HB.tk(kc), w=bta)
                            bg, btg = self.bank("a")
                            for kc in range(KC):
                                self.mm(self.PS.t[:, bg, :n], WI.t[:, kc, 768 + c * P:768 + (c + 1) * P], HB.t[:, kc, :n],
                                        kc == 0, kc == KC - 1, r=WI.tk(kc, 768 + c * P, 768 + (c + 1) * P) + HB.tk(kc), w=btg)
                            sg, sgtk = self.tmpf()
                            self.A(sg[:, :n], self.PS.t[:, bg, :n], AF.Sigmoid, r=btg, w=sgtk)
                            if isctx:
                                dst, dtk = UCC.t[:, c, PADW + c0:PADW + c1], UCC.tk(c)
                            else:
                                dst, dtk = UC.t[:, c, PADW + c0 - CTXL:PADW + c1 - CTXL], UC.tk(c, PADW + c0 - CTXL, PADW + c1 - CTXL)
                            self.E("dve", "tensor_tensor", dst, self.PS.t[:, ba, :n], sg[:, :n], ALU.mult,
                                   r=bta + sgtk, w=dtk)
                        for t in range(n // P):
                            bf, btf = self.bank("a")
                            for kc in range(KC):
                                self.mm(self.PS.t[:, bf, 0:256], HB.t[:, kc, t * P:(t + 1) * P], WI.t[:, kc, 1536:1792],
                                        kc == 0, kc == KC - 1, r=WI.tk(kc, 1536, 1792) + HB.tk(kc), w=btf)
                            tt = c0 // P + t
                            self.A(FT.t[:, tt, :], self.PS.t[:, bf, 0:256], AF.Identity, r=btf, w=FT.tk(tt))
                    S.barrier()
                self.ckpt("od_ip")
                with contextlib.ExitStack() as es2:
                    DN = [self.sb(es2, "DN%d" % k, [P, 2, 4, 512], BF16, gran=1 << 20) for k in range(3)]
                    for b in DN:
                        b.nk = 1
                    D64 = self.sb(es2, "D64", [P, 2, 512], BF16, gran=1 << 20)
                    D64.nk = 1
                    ZB = self.sb(es2, "ZB", [P, 4, 512], BF16, gran=512)
                    S.dma("sp", D64.t[:, :, :], self.d["dft64"].rearrange("(k p) n -> p k n", p=P), writes=D64.tk())
                    ndn = 0
                    streams = [("dftN", SEQ, CTXL, 16)]
                    if need_ctx:
                        streams = [("dftC", CTXL, 0, 2)] + streams
                    for (dname, L, tbase, ntile) in streams:
                        for nb in range(0, L, 512):
                            n = min(512, L - nb)
                            banks = [self.bank("a") for _ in range(4)]
                            for m0 in range(0, ntile, 4):
                                mt = min(4, ntile - m0)
                                dn = DN[ndn % 3]
                                ndn += 1
                                for cs in range(2):
                                    S.dma("sp", dn.t[:, cs, 0:mt, :n],
                                          self.d[dname][cs, m0 * P:(m0 + mt) * P, nb:nb + n].rearrange("(m p) n -> p m n", p=P),
                                          writes=dn.tk())
                                for cs in range(2):
                                    for ch in range(2):
                                        bk, btk = banks[cs * 2 + ch]
                                        for m in range(mt):
                                            tt = tbase // P + m0 + m
                                            self.mm(self.PS.t[:, bk, :n], FT.t[:, tt, ch * P:(ch + 1) * P], dn.t[:, cs, m, :n],
                                                    (m0 + m) == 0, (m0 + m) == ntile - 1, r=FT.tk(tt) + dn.tk(), w=btk)
                            for cs in range(2):
                                for ch in range(2):
                                    bk, btk = banks[cs * 2 + ch]
                                    self.A(ZB.t[:, cs * 2 + ch, :n], self.PS.t[:, bk, :n], AF.Identity, r=btk, w=ZB.tk(cs * 2 + ch))
                            for ch in range(2):
                                bo, bto = self.bank("b")
                                for cs in range(2):
                                    self.mm(self.PS.t[:, bo, :n], D64.t[:, ch, cs * 256 + ch * P:cs * 256 + (ch + 1) * P],
                                            ZB.t[:, cs * 2 + ch, :n], cs == 0, cs == 1,
                                            r=D64.tk() + ZB.tk(cs * 2 + ch), w=bto)
                                self.A(MO.t[:, 6 + ch, tbase + nb:tbase + nb + n], self.PS.t[:, bo, :n], AF.Identity,
                                       r=bto, w=MO.tk(6 + ch, tbase + nb, tbase + nb + n))
                    S.barrier()
            self.ckpt("od_fn")
            with contextlib.ExitStack() as es1:
                CV = self.sb(es1, "CV", [P, 6, TT], BF16)
                DG = [self.sb(es1, "DGo%d" % k, [P, 31, P], BF16, gran=1 << 20) for k in range(2)]
                for b in DG:
                    b.nk = 1
                for c in range(6):
                    dg = DG[c % 2]
                    for k in range(31):
                        self.E("dve", "tensor_scalar", dg.t[:, k, :], self.IDENT.t[:, :],
                               self.col("cdw%d" % i, c * 31 + k, 1), self.CST.t[:, 2:3], ALU.mult, ALU.add,
                               r=self.IDENT.tk() + self.COLS.tk(), w=dg.tk())
                    for (c0, c1) in blocks:
                        n = c1 - c0
                        isctx = c1 <= CTXL
                        bk, btk = self.bank("a")
                        for k in range(31):
                            if isctx:
                                rhs, rtk = UCC.t[:, c, c0 + k:c0 + k + n], UCC.tk(c)
                            else:
                                a0 = c0 - CTXL + k
                                rhs, rtk = UC.t[:, c, a0:a0 + n], UC.tk(c, a0, a0 + n)
                            self.mm(self.PS.t[:, bk, :n], dg.t[:, k, :], rhs, k == 0, k == 30, r=dg.tk() + rtk, w=btk)
                        self.A(CV.t[:, c, c0:c1], self.PS.t[:, bk, :n], AF.Identity, bias=self.col("cdb%d" % i, c, 1),
                               r=btk + self.COLS.tk(), w=CV.tk(c, c0, c1))
                self.ckpt("od_cv")
                for (c0, c1) in blocks:
                    n = c1 - c0
                    b1, bt1 = self.bank("b")
                    b2, bt2 = self.bank("b")
                    for c in range(6):
                        self.mm(self.PS.t[:, b1, :n], self.ONES.t[:, :], CV.t[:, c, c0:c1], c == 0, c == 5,
                                r=CV.tk(c, c0, c1) + self.ONES.tk(), w=bt1)
                    for c in range(6):
                        sq, sqtk = self.tmpb()
                        self.A(sq[:, :n], CV.t[:, c, c0:c1], AF.Square, r=CV.tk(c, c0, c1), w=sqtk)
                        self.mm(self.PS.t[:, b2, :n], self.ONES.t[:, :], sq[:, :n], c == 0, c == 5,
                                r=sqtk + self.ONES.tk(), w=bt2)
                    mu, mutk = self.tmpr()
                    self.A(mu[:, :n], self.PS.t[:, b1, :n], AF.Identity, scale=1.0 / 768, r=bt1, w=mutk)
                    var, vtk = self.tmpf()
                    self.E("dve", "tensor_tensor", var[:, :n], mu[:, :n], mu[:, :n], ALU.mult, r=mutk, w=vtk)
                    self.E("dve", "scalar_tensor_tensor", var[:, :n], self.PS.t[:, b2, :n], self.CST.t[:, 1:2], var[:, :n],
                           ALU.mult, ALU.subtract, r=bt2 + vtk + self.CST.tk(), w=vtk)
                    rs, rtk = self.tmpr()
                    self.A(var[:, :n], var[:, :n], AF.Ln, bias=EPS, scale=1.0, r=vtk, w=vtk)
                    self.A(rs[:, :n], var[:, :n], AF.Exp, scale=-0.5, r=vtk, w=rtk)
                    for c in range(6):
                        t1, t1k = self.tmpf()
                        self.E("dve", "tensor_tensor", t1[:, :n], CV.t[:, c, c0:c1], mu[:, :n], ALU.subtract,
                               r=CV.tk(c, c0, c1) + mutk, w=t1k)
                        self.E("dve", "tensor_tensor", t1[:, :n], t1[:, :n], rs[:, :n], ALU.mult, r=t1k + rtk, w=t1k)
                        self.E("dve", "tensor_scalar", t1[:, :n], t1[:, :n], self.col("lng%d" % i, c, 1),
                               self.col("lnb%d" % i, c, 1), ALU.mult, ALU.add, r=t1k + self.COLS.tk(), w=t1k)
                        self.A(MO.t[:, c, c0:c1], t1[:, :n], AF.Silu, r=t1k, w=MO.tk(c, c0, c1))
                S.barrier()
            self.ckpt("od_ln")
            self.out_proj(l, "od_out_w", i, MO, blocks, es)

    def even_mixer(self, l, need_ctx):
        S = self.S
        i = l // 2
        with contextlib.ExitStack() as es:
            MO = self.sb(es, "MOe", [P, KC, TT], BF16)
            RS = None
            self.mla(l, i, need_ctx, MO, RS)
            if self.stop or self.ckpt("mla"):
                if "mo%d" % l in self.dumps:
                    self.dump("mo%d" % l, MO.t[:, 0:4, :], MO.tk(), [P, 4, TT])
                return
            for jp in range(2):
                self.gla_pair(l, i, jp, need_ctx, MO, RS)
                if self.stop or self.ckpt("gla%d" % jp):
                    return
            blocks = BLOCKS_ALL if need_ctx else BLOCKS_ALL[1:]
            if "mo%d" % l in self.dumps:
                self.dump("mo%d" % l, MO.t[:, :, :], MO.tk(), [P, KC, TT])
            self.out_proj(l, "ev_out_w", i, MO, blocks, es)

    def mla(self, l, i, need_ctx, MO, RS):
        S = self.S
        qblocks = BLOCKS_ALL if need_ctx else BLOCKS_ALL[1:]
        with contextlib.ExitStack() as es:
            QC = self.sb(es, "QC", [P, 2, TT], BF16)
            KVC = self.sb(es, "KVC", [P, TT], BF16)
            KR = self.sb(es, "KR", [P, TT], BF16)
            RSQ = self.sb(es, "RSQ", [P, TT], BF16)
            RSKV = self.sb(es, "RSKV", [P, TT], BF16)
            RSKVT = self.sb(es, "RSKVT", [P, 18], F32, gran=1 << 20)
            ROPE = self.sb(es, "ROPE", [P, 2, SEQ], BF16)
            S.dma("sp", ROPE.t[64:96, :, :], self.d["ropeT"], writes=ROPE.tk())
            with contextlib.ExitStack() as es1:
                HBs = [self.sb(es1, "HBm%d" % k, [P, KC, 512], BF16, gran=512) for k in range(2)]
                WA = self.sb(es1, "WA", [P, KC, 416], BF16, gran=1 << 20)
                WP = self.sb(es1, "WPk", [P, KC, 96], BF16, gran=1 << 20)
                S.dma("pool", WA.t[:, :, :], self.d["ev_in_w"][i, :, 0:416].rearrange("(k p) n -> p k n", p=P), writes=WA.tk())
                S.dma("pool", WP.t[:, :, :], self.d["w_krp"][i].rearrange("(k p) n -> p k n", p=P), writes=WP.tk())
                import os
                ipst = int(os.environ.get("KB_IP", "99"))
                self.hblock(l, BLOCKS_ALL[0][0], BLOCKS_ALL[0][1], 0, 0, None, 0, HBs[0], 0)
                for bi_, (c0, c1) in enumerate(BLOCKS_ALL):
                    HB = HBs[bi_ % 2]
                    n = c1 - c0
                    isctx = c1 <= CTXL
                    if bi_ + 1 < len(BLOCKS_ALL):
                        self.hblock(l, BLOCKS_ALL[bi_ + 1][0], BLOCKS_ALL[bi_ + 1][1], 0, 0, None, 0, HBs[(bi_ + 1) % 2], 0)

                    def proj(col0, ncol, wbuf=WA):
                        bk, btk = self.bank("a")
                        for kc in range(KC):
                            self.mm(self.PS.t[0:ncol, bk, :n], wbuf.t[:, kc, col0:col0 + ncol], HB.t[:, kc, :n],
                                    kc == 0, kc == KC - 1, r=wbuf.tk() + HB.tk(kc), w=btk)
                        return bk, btk
                    if need_ctx or not isctx:
                        bs, bts = self.bank("b")
                        for j in range(2):
                            bk, btk = proj(j * P, P)
                            self.E("dve", "tensor_scalar", QC.t[:, j, c0:c1], self.PS.t[:, bk, :n],
                                   self.col("qn%d" % i, j, 1), self.CST.t[:, 2:3], ALU.mult, ALU.add, r=btk + self.COLS.tk(), w=QC.tk(j, c0, c1))
                            sq, sqtk = self.tmpb()
                            self.A(sq[:, :n], self.PS.t[:, bk, :n], AF.Square, r=btk, w=sqtk)
                            self.mm(self.PS.t[:, bs, :n], self.ONES.t[:, :], sq[:, :n], j == 0, j == 1,
                                    r=sqtk + self.ONES.tk(), w=bts)
                        self.rstd_from_ps(bs, bts, n, 256, RSQ.t[:, c0:c1], RSQ.tk(0, c0, c1))
                    bk, btk = proj(256, P)
                    self.E("dve", "tensor_scalar", KVC.t[:, c0:c1], self.PS.t[:, bk, :n], self.col("kvn%d" % i, 0, 1),
                           self.CST.t[:, 2:3], ALU.mult, ALU.add, r=btk + self.COLS.tk(), w=KVC.tk(0, c0, c1))
                    sq, sqtk = self.tmpb()
                    self.A(sq[:, :n], self.PS.t[:, bk, :n], AF.Square, r=btk, w=sqtk)
                    bs, bts = self.bank("b")
                    self.mm(self.PS.t[:, bs, :n], self.ONES.t[:, :], sq[:, :n], True, True, r=sqtk + self.ONES.tk(), w=bts)
                    self.rstd_from_ps(bs, bts, n, 128, RSKV.t[:, c0:c1], RSKV.tk(0, c0, c1))
                    bs2, bts2 = self.bank("b")
                    for t in range(n // P):
                        self.mm(self.PS.t[:, bs2, t:t + 1], sq[:, t * P:(t + 1) * P], self.ONES.t[:, 0:1], True, True,
                                r=sqtk + self.ONES.tk(), w=bts2)
                    self.rstd_from_ps(bs2, bts2, n // P, 128, RSKVT.t[:, c0 // P:c0 // P + n // P], RSKVT.tk())
                    bk, btk = proj(320, 96)
                    if isctx:
                        self.E("dve", "tensor_copy", KR.t[64:96, c0:c1], self.PS.t[64:96, bk, :n], r=btk, w=KR.tk(0, c0, c1))
                    else:
                        bk2, btk2 = proj(0, 96, WP)
                        t1, t1k = self.tmpf()
                        t2, t2k = self.tmpf()
                        a0 = c0 - CTXL
                        self.E("dve", "tensor_tensor", t1[64:96, :n], self.PS.t[64:96, bk, :n], ROPE.t[64:96, 0, a0:a0 + n],
                               ALU.mult, r=btk + ROPE.tk(), w=t1k)
                        self.E("dve", "tensor_tensor", t2[64:96, :n], self.PS.t[64:96, bk2, :n], ROPE.t[64:96, 1, a0:a0 + n],
                               ALU.mult, r=btk2 + ROPE.tk(), w=t2k)
                        self.E("dve", "tensor_tensor", KR.t[64:96, c0:c1], t1[64:96, :n], t2[64:96, :n], ALU.add,
                               r=t1k + t2k, w=KR.tk(0, c0, c1))
                if "mlaip" in self.dumps:
                    self.dump("mod", self.MOD.t[:, :, :, :].rearrange("p l a b -> p (l a b)"), self.MOD.tk(), [P, DEPTH * 96])
                    self.dump("der", self.DER.t[:, :, :].rearrange("p a b -> p (a b)"), self.DER.tk(), [P, DEPTH * 64])
                    self.dump("hb", HBs[0].t[:, :, :], HBs[0].tk(), [P, KC, 512])
                    self.dump("qc", QC.t[:, :, :], QC.tk(), [P, 2, TT])
                    self.dump("kvc", KVC.t[:, :], KVC.tk(), [P, TT])
                    self.dump("rsq", RSQ.t[:, :], RSQ.tk(), [P, TT])
                    self.dump("rskv", RSKV.t[:, :], RSKV.tk(), [P, TT])
                    self.dump("rskvt", RSKVT.t[:, :], RSKVT.tk(), [P, 18])
                    self.dump("kr", KR.t[64:96, :], KR.tk(), [32, TT])
                S.barrier()
            if self.ckpt("mla_inproj"):
                return
            with contextlib.ExitStack() as es1:
                WUQ = self.sb(es1, "WUQ", [P, 2, 768], BF16, gran=1 << 20)
                WUQP = self.sb(es1, "WUQP", [P, 2, 768], BF16, gran=1 << 20)
                WUKV = self.sb(es1, "WUKV", [P, 1024], BF16, gran=1 << 20)
                KTs = [self.sb(es1, "KT%d" % k, [P, 2, TT], BF16) for k in range(1)]
                VAs = [self.sb(es1, "VA%d" % k, [P, 18, 256], BF16, gran=128) for k in range(1)]
                QT = [self.sb(es1, "QT%d" % k, [P, 512], BF16, gran=512) for k in range(2)]
                PT = [self.sb(es1, "PT%d" % k, [P, 2, 512], BF16, gran=512) for k in range(3)]
                S.dma("pool", WUQ.t[:, :, :], self.d["w_uq"][i].rearrange("(k p) n -> p k n", p=P), writes=WUQ.tk())
                S.dma("pool", WUQP.t[:, :, :], self.d["w_uqp"][i].rearrange("(k p) n -> p k n", p=P), writes=WUQP.tk())
                S.dma("pool", WUKV.t[:, :], self.d["w_ukv"][i], writes=WUKV.tk())
                self.convert_ffn_weights(l)
                for VA_ in VAs:
                    self.E("pool", "memset", VA_.t[:, :, 64:192], 1.0, w=VA_.tk())
                npt = 0
                npair = 0

                def kvprep(j, hh, KT, VA):
                    h = 2 * j + hh
                    for (c0, c1) in BLOCKS_ALL:
                        n = c1 - c0
                        bk, btk = self.bank("b")
                        self.mm(self.PS.t[0:64, bk, :n], WUKV.t[:, h * P:h * P + 64], KVC.t[:, c0:c1], True, True,
                                r=WUKV.tk() + KVC.tk(0, c0, c1), w=btk)
                        self.E("dve", "tensor_tensor", KT.t[0:64, hh, c0:c1], self.PS.t[0:64, bk, :n], RSKV.t[0:64, c0:c1],
                               ALU.mult, r=btk + RSKV.tk(0, c0, c1), w=KT.tk(hh, c0, c1))
                    self.E("pool", "tensor_copy", KT.t[64:96, hh, :], KR.t[64:96, :], r=KR.tk(), w=KT.tk(hh))
                    vc0 = 0 if hh == 0 else 192
                    for t4 in range(0, 18, 4):
                        bk, btk = self.bank("b")
                        tl = list(range(t4, min(18, t4 + 4)))
                        for q_, t in enumerate(tl):
                            self.mm(self.PS.t[:, bk, q_ * 64:(q_ + 1) * 64], KVC.t[:, t * P:(t + 1) * P],
                                    WUKV.t[:, h * P + 64:(h + 1) * P], True, True,
                                    r=WUKV.tk() + KVC.tk(0, t * P, (t + 1) * P), w=btk)
                        for q_, t in enumerate(tl):
                            self.E("dve", "tensor_scalar", VA.t[:, t, vc0:vc0 + 64], self.PS.t[:, bk, q_ * 64:(q_ + 1) * 64],
                                   RSKVT.t[:, t:t + 1], self.CST.t[:, 2:3], ALU.mult, ALU.add, r=btk + RSKVT.tk(),
                                   w=VA.tk(t, hh * P, (hh + 1) * P))
                kvprep(0, 0, KTs[0], VAs[0])
                kvprep(0, 1, KTs[0], VAs[0])
                for j in range(4):
                    KT, VA = KTs[0], VAs[0]
                    items = [(hh, c0, c1) for hh in range(2) for (c0, c1) in qblocks]

                    def qprep(item, qt):
                        hh, c0, c1 = item
                        h = 2 * j + hh
                        n = c1 - c0
                        isctx = c1 <= CTXL
                        bq, btq = self.bank("b")
                        for kc in range(2):
                            self.mm(self.PS.t[0:96, bq, :n], WUQ.t[:, kc, h * 96:(h + 1) * 96], QC.t[:, kc, c0:c1],
                                    kc == 0, kc == 1, r=WUQ.tk() + QC.tk(kc, c0, c1), w=btq)
                        if isctx:
                            self.E("dve", "tensor_tensor", qt.t[0:96, :n], self.PS.t[0:96, bq, :n], RSQ.t[0:96, c0:c1],
                                   ALU.mult, r=btq + RSQ.tk(0, c0, c1), w=qt.tk())
                        else:
                            a0 = c0 - CTXL
                            self.E("dve", "tensor_tensor", qt.t[0:64, :n], self.PS.t[0:64, bq, :n], RSQ.t[0:64, c0:c1],
                                   ALU.mult, r=btq + RSQ.tk(0, c0, c1), w=qt.tk())
                            bq2, btq2 = self.bank("b")
                            for kc in range(2):
                                self.mm(self.PS.t[0:96, bq2, :n], WUQP.t[:, kc, h * 96:(h + 1) * 96], QC.t[:, kc, c0:c1],
                                        kc == 0, kc == 1, r=WUQP.tk() + QC.tk(kc, c0, c1), w=btq2)
                            t1, t1k = self.tmpf()
                            t2, t2k = self.tmpf()
                            self.E("dve", "tensor_tensor", t1[64:96, :n], self.PS.t[64:96, bq, :n], ROPE.t[64:96, 0, a0:a0 + n],
                                   ALU.mult, r=btq + ROPE.tk(), w=t1k)
                            self.E("dve", "tensor_tensor", t2[64:96, :n], self.PS.t[64:96, bq2, :n], ROPE.t[64:96, 1, a0:a0 + n],
                                   ALU.mult, r=btq2 + ROPE.tk(), w=t2k)
                            self.E("pool", "tensor_tensor", t1[64:96, :n], t1[64:96, :n], t2[64:96, :n], ALU.add,
                                   r=t1k + t2k, w=t1k)
                            self.E("pool", "tensor_tensor", qt.t[64:96, :n], t1[64:96, :n], RSQ.t[64:96, c0:c1], ALU.mult,
                                   r=t1k + RSQ.tk(0, c0, c1), w=qt.tk())

                    def attend(item, qt):
                        nonlocal npt
                        hh, c0, c1 = item
                        n = c1 - c0
                        isctx = c1 <= CTXL
                        ktiles = list(range(2)) if isctx else list(range(18))
                        bo, bto = self.bank("c")
                        pend = []
                        kpairs = [(ktiles[q], ktiles[q + 1]) for q in range(0, len(ktiles), 2)]

                        def s_issue(pr):
                            nonlocal npair
                            b0 = (npair % 2) * 2
                            npair += 1
                            btk2 = self.PS.tk(b0) + self.PS.tk(b0 + 1)
                            for k, t in enumerate(pr):
                                self.mm(self.PS.t[:, b0 + k, :n], KT.t[0:96, hh, t * P:(t + 1) * P], qt.t[0:96, :n], True, True,
                                        r=KT.tk(hh, t * P, (t + 1) * P) + qt.tk(), w=self.PS.tk(b0 + k), rg=(0, 96))
                            pend.append((pr, b0, btk2))

                        def finish():
                            nonlocal npt
                            pr, b0, btk2 = pend.pop(0)
                            pt = PT[npt % 3]
                            npt += 1
                            self.A(pt.t[:, :, :n], self.PS.t[:, b0:b0 + 2, :n], AF.Exp, scale=MLA_SCALE, r=btk2, w=pt.tk())
                            for k, t in enumerate(pr):
                                self.mm(self.PS.t[:, bo, :n], VA.t[:, t, hh * P:(hh + 1) * P], pt.t[:, k, :n],
                                        t == ktiles[0], t == ktiles[-1], r=VA.tk(t, hh * P, (hh + 1) * P) + pt.tk(), w=bto)
                        s_issue(kpairs[0])
                        for q in range(len(kpairs)):
                            if q + 1 < len(kpairs):
                                s_issue(kpairs[q + 1])
                            finish()
                        dn, dtk = self.tmpf()
                        if hh == 0:
                            self.E("dve", "reciprocal", dn[0:64, :n], self.PS.t[64:128, bo, :n], r=bto, w=dtk)
                            self.E("dve", "tensor_tensor", MO.t[0:64, j, c0:c1], self.PS.t[0:64, bo, :n], dn[0:64, :n],
                                   ALU.mult, r=bto + dtk, w=MO.tk(j, c0, c1))
                        else:
                            self.E("dve", "reciprocal", dn[64:128, :n], self.PS.t[0:64, bo, :n], r=bto, w=dtk)
                            self.E("dve", "tensor_tensor", MO.t[64:128, j, c0:c1], self.PS.t[64:128, bo, :n], dn[64:128, :n],
                                   ALU.mult, r=bto + dtk, w=MO.tk(j, c0, c1))
                    qprep(items[0], QT[0])
                    for k, item in enumerate(items):
                        if k + 1 < len(items):
                            qprep(items[k + 1], QT[(k + 1) % 2])
                        attend(item, QT[k % 2])
                        last_of_head = (k + 1 == len(items)) or (items[k + 1][0] != item[0])
                        if last_of_head and j + 1 < 4:
                            kvprep(j + 1, item[0], KT, VA)
                S.barrier()

    def gla_pair(self, l, i, jp, need_ctx, MO, RS):
        S = self.S
        NT = 18
        with contextlib.ExitStack() as es:
            GQ = self.sb(es, "GQ", [P, TT], BF16)
            GK = self.sb(es, "GK", [P, TT], BF16)
            GKT = self.sb(es, "GKT", [P, NT, P], BF16, gran=1 << 20)
            GVT = self.sb(es, "GVT", [P, NT, 256], BF16, gran=1 << 20)
            GLR = self.sb(es, "GLR", [P, TT], BF16)
            SG = self.sb(es, "SG", [P, 2, TT], BF16)
            self.E("pool", "memset", GLR.t[32:64, :], 1.0, w=GLR.tk())
            with contextlib.ExitStack() as es1:
                HBs = [self.sb(es1, "HBg%d" % k, [P, KC, 512], BF16, gran=512) for k in range(2)]
                WGq = self.sb(es1, "WGq", [P, KC, 800], BF16, gran=1 << 20)
                srcs = [(0, 416 + jp * P, P), (128, 672 + jp * P, P), (256, 928 + jp * 256, 256), (512, 1440, 32),
                        (544, 1472 + jp * 256, 256)]
                S.dma("pool", WGq.t[:, :, 0:512], self.d["ev_gla"][i, jp][:, :, 0:512], writes=WGq.tk())
                S.dma("pool", WGq.t[:, :, 512:800], self.d["ev_gla"][i, jp][:, :, 512:800], writes=WGq.tk())
                self.hblock(l, BLOCKS_ALL[0][0], BLOCKS_ALL[0][1], 0, 0, None, 0, HBs[0], 0)
                for bi_, (c0, c1) in enumerate(BLOCKS_ALL):
                    HB = HBs[bi_ % 2]
                    n = c1 - c0
                    if bi_ + 1 < len(BLOCKS_ALL):
                        self.hblock(l, BLOCKS_ALL[bi_ + 1][0], BLOCKS_ALL[bi_ + 1][1], 0, 0, None, 0, HBs[(bi_ + 1) % 2], 0)

                    def proj(col0, ncol):
                        bk, btk = self.bank("a")
                        for kc in range(KC):
                            self.mm(self.PS.t[0:ncol, bk, :n], WGq.t[:, kc, col0:col0 + ncol], HB.t[:, kc, :n],
                                    kc == 0, kc == KC - 1, r=WGq.tk() + HB.tk(kc), w=btk)
                        return bk, btk
                    bk, btk = proj(0, P)
                    self.A(GQ.t[:, c0:c1], self.PS.t[:, bk, :n], AF.Identity, scale=0.125, r=btk, w=GQ.tk(0, c0, c1))
                    bk, btk = proj(128, P)
                    self.A(GK.t[:, c0:c1], self.PS.t[:, bk, :n], AF.Identity, r=btk, w=GK.tk(0, c0, c1))
                    bk, btk = proj(512, 32)
                    self.E("dve", "tensor_copy", GLR.t[0:32, c0:c1], self.PS.t[0:32, bk, :n], r=btk, w=GLR.tk(0, c0, c1))
                    for cc in range(2):
                        bk, btk = proj(544 + cc * P, P)
                        self.A(SG.t[:, cc, c0:c1], self.PS.t[:, bk, :n], AF.Silu, r=btk, w=SG.tk(cc, c0, c1))
                    for t in range(n // P):
                        tt = c0 // P + t
                        bk, btk = self.bank("a")
                        for kc in range(KC):
                            self.mm(self.PS.t[:, bk, 0:384], HB.t[:, kc, t * P:(t + 1) * P], WGq.t[:, kc, 128:512],
                                    kc == 0, kc == KC - 1, r=WGq.tk() + HB.tk(kc), w=btk)
                        self.A(GKT.t[:, tt, :], self.PS.t[:, bk, 0:128], AF.Identity, r=btk, w=GKT.tk(tt))
                        self.E("dve", "tensor_copy", GVT.t[:, tt, :], self.PS.t[:, bk, 128:384], r=btk, w=GVT.tk(tt))
                S.barrier()
            if self.ckpt("gla_inproj"):
                return
            with contextlib.ExitStack() as es1:
                WGT = self.sb(es1, "WGT", [P, 512], BF16, gran=1 << 20)
                MSK = self.sb(es1, "MSK", [P, 4, P], F32, gran=1 << 20)
                MSK.nk = 1
                MK2 = self.sb(es1, "MK2", [P, 2, P], BF16, gran=1 << 20)
                MK2.nk = 1
                QTL = self.sb(es1, "QTL", [P, TT], BF16)
                KTL = self.sb(es1, "KTL", [P, TT], BF16)
                KH = self.sb(es1, "KH", [P, NT, P], BF16, gran=1 << 20)
                EBE = self.sb(es1, "EBE", [P, 36], F32, gran=1 << 20)
                OA = self.sb(es1, "OA", [P, 2, TT], BF16)
                SF = self.sb(es1, "SF", [P, P], F32, gran=1 << 20)
                SBB = [self.sb(es1, "SBb%d" % k, [P, P], BF16, gran=1 << 20) for k in range(2)]
                AM = [self.sb(es1, "AM%d" % k, [P, 2, P], BF16, gran=1 << 20) for k in range(2)]
                LG = [self.sb(es1, "LG%d" % k, [P, 2 * P], F32, gran=1 << 20) for k in range(2)]
                S.dma("pool", WGT.t[0:64, :], self.d["w_gate"][i], writes=WGT.tk())
                S.dma("sp", MSK.t[:, :, :], self.d["masks"][:, 0:4, :], writes=MSK.tk())
                S.dma("pool", MK2.t[:, :, :], self.d["masks"][:, 4:6, :], writes=MK2.tk())
                nlg = nam = 0
                import os
                glst = int(os.environ.get("KB_GL", "99"))
                for d in range(2 if glst >= 2 else 0):
                    for t in range(0, NT, 2):
                        c2 = slice(t * P, (t + 2) * P)
                        ctk = (0, t * P, (t + 2) * P)
                        bz, btz = self.bank("a")
                        gcol = d * 256 + jp * P
                        for u in range(2):
                            cs = slice((t + u) * P, (t + u + 1) * P)
                            self.mm(self.PS.t[:, bz, u * P:(u + 1) * P], GLR.t[0:33, cs], WGT.t[0:33, gcol:gcol + P], True, True,
                                    r=GLR.tk(*ctk) + WGT.tk(), w=btz, rg=(0, 33))
                        lg = LG[nlg % 2]
                        nlg += 1
                        self.A(lg.t[:, :], self.PS.t[:, bz, 0:2 * P], AF.Exp, scale=-1.0, r=btz, w=lg.tk())
                        self.A(lg.t[:, :], lg.t[:, :], AF.Ln, bias=1.0, scale=1.0, r=lg.tk(), w=lg.tk())
                        bb, btb = self.bank("a")
                        for u in range(2):
                            lgu = lg.t[:, u * P:(u + 1) * P]
                            self.mm(self.PS.t[:, bb, u * P:(u + 1) * P], lgu, MSK.t[:, d, :], True, True,
                                    r=lg.tk() + MSK.tk(), w=btb)
                            self.mm(self.PS.t[:, bb, (2 + u) * P:(3 + u) * P], MSK.t[:, 2 + d, :], lgu, True, True,
                                    r=lg.tk() + MSK.tk(), w=btb)
                        eb, ebk = self.tmpf()
                        eb2, ebk2 = self.tmpf()
                        self.A(eb[:, 0:2 * P], self.PS.t[:, bb, 0:2 * P], AF.Exp, r=btb, w=ebk)
                        self.A(eb[:, 2 * P:4 * P], self.PS.t[:, bb, 0:2 * P], AF.Exp, scale=-1.0, r=btb, w=ebk)
                        self.A(eb2[:, 0:2 * P], self.PS.t[:, bb, 2 * P:4 * P], AF.Exp, r=btb, w=ebk2)
                        self.E("dve", "tensor_tensor", QTL.t[:, c2], GQ.t[:, c2], eb[:, 0:2 * P], ALU.mult,
                               r=GQ.tk(*ctk) + ebk, w=QTL.tk(*ctk))
                        self.E("dve", "tensor_tensor", KTL.t[:, c2], GK.t[:, c2], eb[:, 2 * P:4 * P], ALU.mult,
                               r=GK.tk(*ctk) + ebk, w=KTL.tk(*ctk))
                        self.E("dve", "tensor_tensor", KH.t[:, t:t + 2, :].rearrange("p a b -> p (a b)"),
                               GKT.t[:, t:t + 2, :].rearrange("p a b -> p (a b)"), eb2[:, 0:2 * P], ALU.mult,
                               r=GKT.tk(t) + GKT.tk(t + 1) + ebk2, w=KH.tk(t) + KH.tk(t + 1))
                        e0 = 63 if d == 0 else 0
                        self.E("dve", "tensor_copy", EBE.t[:, 2 * t:2 * t + 4], eb[:, e0:e0 + 193:64], r=ebk, w=EBE.tk())
                    self.ckpt("gla_gate")
                    if glst == 2 or (glst == 3 and d == 1):
                        continue
                    self.E("pool", "memset", SF.t[:, :], 0.0, w=SF.tk())
                    self.E("pool", "memset", SBB[0].t[:, :], 0.0, w=SBB[0].tk())
                    if d == 0:
                        order = list(range(36))
                    else:
                        order = [3, 2, 1, 0] + list(range(35, 3, -1))
                    kvq = []

                    def kv_issue(c):
                        t = c // 2
                        p0 = (c % 2) * 64
                        bs, bts = self.bank("a")
                        for hh in range(2):
                            r0 = hh * 64
                            self.mm(self.PS.t[r0:r0 + 64, bs, 0:P], KH.t[p0:p0 + 64, t, r0:r0 + 64],
                                    GVT.t[p0:p0 + 64, t, hh * P:(hh + 1) * P], True, True, r=KH.tk(t) + GVT.tk(t), w=bts,
                                    rg=(p0, 64))
                        kvq.append((bs, bts))
                    am = None
                    am_tile = -1
                    LOOK = 2
                    for c in order[:LOOK]:
                        kv_issue(c)
                    for si_, c in enumerate(order):
                        if si_ + LOOK < len(order):
                            kv_issue(order[si_ + LOOK])
                        t = c // 2
                        p0 = (c % 2) * 64
                        cols = slice(c * 64, (c + 1) * 64)
                        sb_cur = SBB[si_ % 2]
                        sb_nxt = SBB[(si_ + 1) % 2]
                        if t != am_tile:
                            am = AM[nam % 2]
                            nam += 1
                            am_tile = t
                            ts_ = slice(t * P, (t + 1) * P)
                            for hh in range(2):
                                r0 = hh * 64
                                ba, bta = self.bank("b")
                                self.mm(self.PS.t[:, ba, 0:P], KTL.t[r0:r0 + 64, ts_], QTL.t[r0:r0 + 64, ts_],
                                        True, True, r=KTL.tk(0, t * P, (t + 1) * P) + QTL.tk(0, t * P, (t + 1) * P), w=bta,
                                        rg=(r0, 64))
                                self.E("dve", "tensor_tensor", am.t[:, hh, :], self.PS.t[:, ba, 0:P],
                                       MK2.t[:, d, :], ALU.mult, r=bta + MK2.tk(), w=am.tk())
                        need_o = need_ctx or c >= 4
                        if need_o:
                            for hh in range(2):
                                r0 = hh * 64
                                bo, bto = self.bank("c")
                                self.mm(self.PS.t[:, bo, 0:64], sb_cur.t[r0:r0 + 64, :], QTL.t[r0:r0 + 64, cols],
                                        True, False, r=sb_cur.tk() + QTL.tk(0, c * 64, (c + 1) * 64), w=bto, rg=(r0, 64))
                                self.mm(self.PS.t[:, bo, 0:64], GVT.t[p0:p0 + 64, t, hh * P:(hh + 1) * P],
                                        am.t[p0:p0 + 64, hh, p0:p0 + 64], False, True, r=GVT.tk(t) + am.tk(), w=bto, rg=(p0, 64))
                                if d == 0:
                                    self.A(OA.t[:, hh, cols], self.PS.t[:, bo, 0:64], AF.Identity, r=bto,
                                           w=OA.tk(hh, c * 64, (c + 1) * 64))
                                else:
                                    self.E("dve", "tensor_tensor", OA.t[:, hh, cols], self.PS.t[:, bo, 0:64],
                                           OA.t[:, hh, cols], ALU.add, r=bto + OA.tk(hh, c * 64, (c + 1) * 64),
                                           w=OA.tk(hh, c * 64, (c + 1) * 64))
                        bs, bts = kvq.pop(0)
                        self.E("dve", "tensor_scalar", SF.t[:, :], SF.t[:, :], EBE.t[:, c:c + 1], self.CST.t[:, 2:3],
                               ALU.mult, ALU.add, r=SF.tk() + EBE.tk(), w=SF.tk())
                        self.E("dve", "tensor_tensor", sb_nxt.t[:, :], self.PS.t[:, bs, 0:P], SF.t[:, :], ALU.add,
                               r=SF.tk() + bts, w=sb_nxt.tk())
                        self.E("dve", "tensor_tensor", SF.t[:, :], self.PS.t[:, bs, 0:P], SF.t[:, :], ALU.add,
                               r=SF.tk() + bts, w=SF.tk())
                oblocks = BLOCKS_ALL if need_ctx else BLOCKS_ALL[1:]
                if glst < 99:
                    oblocks = []
                for (c0, c1) in oblocks:
                    n = c1 - c0
                    for hh in range(2):
                        rs, rtk = self.tmpf()
                        self.stats(lambda kc: (OA.t[:, hh, c0:c1], OA.tk(hh, c0, c1)), n, 1, 128, rs[:, :n], rtk)
                        t1, t1k = self.tmpf()
                        self.E("dve", "tensor_tensor", t1[:, :n], OA.t[:, hh, c0:c1], rs[:, :n], ALU.mult,
                               r=OA.tk(hh, c0, c1) + rtk, w=t1k)
                        self.E("dve", "scalar_tensor_tensor", MO.t[:, 4 + 2 * jp + hh, c0:c1], t1[:, :n], self.col("on%d" % i, 0, 1),
                               SG.t[:, hh, c0:c1], ALU.mult, ALU.mult, r=t1k + self.COLS.tk() + SG.tk(hh, c0, c1),
                               w=MO.tk(4 + 2 * jp + hh, c0, c1))
                S.barrier()


def _consts():
    c = {}
    c["ident"] = np.eye(P, dtype=np.float32)
    n = SEQ
    row = np.repeat(np.arange(n // GRID_W), GRID_W).astype(np.float64)
    colp = np.tile(np.arange(GRID_W), n // GRID_W).astype(np.float64)
    inv = ROPE_BASE ** (-np.arange(0, 16, 2, dtype=np.float64) / 16)
    ar = row[:, None] * inv.astype(np.float32).astype(np.float64)
    ac = colp[:, None] * inv.astype(np.float32).astype(np.float64)
    ang = np.concatenate([ar, ar, ac, ac], axis=-1)
    sign = np.concatenate([-np.ones(8), np.ones(8), -np.ones(8), np.ones(8)])
    rope = np.zeros((32, 2, n), np.float32)
    rope[:, 0, :] = np.cos(ang).T
    rope[:, 1, :] = (np.sin(ang) * sign[None, :]).T
    c["ropeT"] = rope.astype(ml_dtypes.bfloat16)
    j = np.arange(P)[:, None]
    ii = np.arange(P)[None, :]
    same = (j // 64) == (ii // 64)
    m = np.zeros((P, 6, P), np.float32)
    m[:, 0, :] = np.where(same & (j <= ii), -1.0 / 16, 0.0)
    m[:, 1, :] = np.where(same & (j >= ii), -1.0 / 16, 0.0)
    m[:, 2, :] = np.where(same & (j > ii), -1.0 / 16, 0.0)
    m[:, 3, :] = np.where(same & (j < ii), -1.0 / 16, 0.0)
    m[:, 4, :] = np.where(same & (j <= ii), 1.0, 0.0)
    m[:, 5, :] = np.where(same & (j >= ii), 1.0, 0.0)
    c["masks"] = m

    def dft(L):
        k = np.arange(L)
        ph = (np.outer(k, k) % L).astype(np.float64) * (2 * np.pi / L)
        s = 1.0 / np.sqrt(L * 64.0)
        return np.stack([np.cos(ph) * s, -np.sin(ph) * s]).astype(np.float32).astype(ml_dtypes.bfloat16)
    c["dftN"] = dft(SEQ)
    c["dftC"] = dft(CTXL)
    k = np.arange(64)
    ph = (np.outer(k, k) % 64).astype(np.float64) * (2 * np.pi / 64)
    d64 = np.zeros((256, 512), np.float32)
    for g in range(4):
        d64[g * 64:(g + 1) * 64, g * 64:(g + 1) * 64] = np.cos(ph)
        d64[g * 64:(g + 1) * 64, 256 + g * 64:256 + (g + 1) * 64] = np.sin(ph)
    c["dft64"] = d64.astype(ml_dtypes.bfloat16)
    return c


def _colv(v):
    v = np.asarray(v, np.float32)
    return np.ascontiguousarray(v.reshape(-1, P).T)


def _pack_cols(inp):
    cols = np.zeros((P, NCOL), np.float32)

    def put(name, arr):
        o = COFF[name]
        cols[:, o:o + arr.shape[1]] = arr
    for l in range(DEPTH):
        put("modb%d" % l, np.repeat(_colv(inp["mod_b"][l]), 2, axis=1))
        put("pmg%d" % l, _colv(inp["pre_mix_g"][l]))
        put("qmg%d" % l, _colv(inp["post_mix_g"][l]))
        put("pfg%d" % l, _colv(inp["pre_ffn_g"][l]))
        put("qfg%d" % l, _colv(inp["post_ffn_g"][l]))
        put("fdw%d" % l, np.ascontiguousarray(np.asarray(inp["ffn_dw_w"][l]).reshape(3, NFC, P).transpose(2, 1, 0)).reshape(P, 66))
        put("fdb%d" % l, _colv(inp["ffn_dw_b"][l]))
    for i in range(2):
        put("qn%d" % i, _colv(inp["mla_q_norm"][i]))
        put("kvn%d" % i, _colv(inp["mla_kv_norm"][i]))
        put("on%d" % i, _colv(inp["gla_o_norm"][i]))
        put("cdw%d" % i, np.ascontiguousarray(np.asarray(inp["conf_dw_w"][i]).reshape(31, 6, P).transpose(2, 1, 0)).reshape(P, 186))
        put("cdb%d" % i, _colv(inp["conf_dw_b"][i]))
        put("lng%d" % i, _colv(inp["conf_ln_g"][i]))
        put("lnb%d" % i, _colv(inp["conf_ln_b"][i]))
    return cols


_PERM32 = np.concatenate([np.arange(8, 16), np.arange(0, 8), np.arange(24, 32), np.arange(16, 24)])


def _shared_inputs(inp):
    f = lambda a: np.ascontiguousarray(np.asarray(a, dtype=np.float32))
    sh = dict(_consts())
    sh["cols"] = _pack_cols(inp)
    for k in ["mod_w", "ev_in_w"]:
        sh[k] = f(inp[k])
    wi_ = f(inp["od_in_w"]).reshape(2, KC, P, 14, P)
    sh["od_in_w"] = np.ascontiguousarray(wi_.transpose(0, 3, 2, 1, 4))
    for k in ["ev_out_w", "od_out_w"]:
        w_ = f(inp[k]).reshape(2, KC, P, KC, P)
        sh[k] = np.ascontiguousarray(w_.transpose(0, 3, 2, 1, 4))
    wi = f(inp["ffn_in_w"]).reshape(DEPTH, KC, P, 2, NFC, P)
    sh["ffn_wgv"] = np.ascontiguousarray(wi.transpose(0, 4, 2, 1, 3, 5)).reshape(DEPTH, NFC, P, KC, 256)
    wo = f(inp["ffn_out_w"]).reshape(DEPTH, NFC, P, KC, P)
    sh["ffn_wo"] = np.ascontiguousarray(wo.transpose(0, 3, 2, 1, 4))
    sh["w_uq"] = f(inp["mla_w_uq"])
    sh["w_ukv"] = f(inp["mla_w_ukv"])
    ev = f(inp["ev_in_w"])
    krp = np.zeros((2, D, 96), np.float32)
    krp[:, :, 64:96] = ev[:, :, 384 + _PERM32]
    sh["w_krp"] = krp
    uq = f(inp["mla_w_uq"]).reshape(2, 256, 8, 96)
    uqp = np.zeros_like(uq)
    uqp[:, :, :, 64:96] = uq[:, :, :, 64 + _PERM32]
    sh["w_uqp"] = np.ascontiguousarray(uqp.reshape(2, 256, 768))
    gla = np.zeros((2, 2, P, KC, 800), np.float32)
    ev_r = ev.reshape(2, KC, P, 1984)
    for jp in range(2):
        for (o, c, w) in [(0, 416 + jp * P, P), (128, 672 + jp * P, P), (256, 928 + jp * 256, 256), (512, 1440, 32),
                          (544, 1472 + jp * 256, 256)]:
            gla[:, jp, :, :, o:o + w] = ev_r[:, :, :, c:c + w].transpose(0, 2, 1, 3)
    sh["ev_gla"] = gla
    wg = np.zeros((2, 64, 512), np.float32)
    wg[:, 0:16, 0:256] = f(inp["gla_w_gate_fw"])
    wg[:, 16:32, 256:512] = f(inp["gla_w_gate_bw"])
    wg[:, 32, 0:256] = f(inp["gla_b_gate_fw"])
    wg[:, 32, 256:512] = f(inp["gla_b_gate_bw"])
    sh["w_gate"] = wg
    return sh


def _core_inputs(inp, b):
    x = np.asarray(inp["x"], np.float32)
    ctx = np.asarray(inp["ctx"], np.float32)
    cc = np.zeros((P, 8, 2), np.float32)
    cc[:, :, 0] = _colv(inp["c"][b])
    cc[:, :, 1] = _colv(inp["c_ctx"])
    return {"xT": np.ascontiguousarray(x[b].T), "ctxT": np.ascontiguousarray(ctx[b].T),
            "ccol": np.ascontiguousarray(cc.reshape(P, 16))}


_NC_CACHE = {}


def kernel(**inputs):
    inp = {k: np.asarray(v) for k, v in inputs.items()}
    nb = inp["x"].shape[0]
    if "nc" not in _NC_CACHE:
        _NC_CACHE["nc"] = KB().build()
    nc = _NC_CACHE["nc"]
    sh = _shared_inputs(inp)
    in_maps = []
    for b in range(nb):
        m = dict(sh)
        m.update(_core_inputs(inp, b))
        in_maps.append(m)
    res = run_bass_kernel_spmd(nc, in_maps, core_ids=list(range(nb)))
    out = np.stack([np.ascontiguousarray(np.asarray(r["outT"], dtype=np.float32).T) for r in res.results], axis=0)
    return out
```
